# Optimizing a Trainium2 kernel written in Bass

```python
import jax, jax.numpy as jnp
from jax import lax
import numpy as np


D_MODEL = 1024
BATCH = 8
SEQ = 4096
DEPTH = 2

D_FF = 2816
FFN_RES_SCALE = 0.5
RMS_EPS = 1e-6
PLE_DIM = 256
QBLK = 128
A_HEADS = 8
A_KV_HEADS = 2
A_GROUP = A_HEADS // A_KV_HEADS
A_HEAD_DIM = 64
WINDOW = 128
B_HEADS = 8
B_Q_LORA = 256
B_KV_LORA = 128
B_NOPE_DIM = 64
B_ROPE_DIM = 32
B_V_DIM = 64
ROPE_THETA = 10000.0
C_HEADS = 16
C_HEAD_DIM = 64
FORGET_BIAS_CENTER = 3.0
N_EVEN = (DEPTH + 1) // 2
N_ODD = DEPTH // 2
EVEN_IN_SPLITS = (A_HEADS * A_HEAD_DIM, A_KV_HEADS * A_HEAD_DIM, A_KV_HEADS * A_HEAD_DIM, B_Q_LORA, B_KV_LORA, B_ROPE_DIM)
EVEN_IN_DIM = A_HEADS * A_HEAD_DIM + 2 * A_KV_HEADS * A_HEAD_DIM + B_Q_LORA + B_KV_LORA + B_ROPE_DIM
EVEN_MIX_DIM = A_HEADS * A_HEAD_DIM + B_HEADS * B_V_DIM
ODD_MIX_DIM = C_HEADS * C_HEAD_DIM
ODD_IN_DIM = 3 * ODD_MIX_DIM + C_HEADS

kernel_name = 'hybrid_swa_mla_fox_macaron'


def rms_norm(x, g):
    xf = x.astype(jnp.float32)
    y = xf * lax.rsqrt(jnp.mean(xf * xf, axis=-1, keepdims=True) + RMS_EPS)
    return (y * g.astype(jnp.float32)).astype(x.dtype)


def swiglu(x, w_gate_up, w_down):
    g, u = jnp.split(x @ w_gate_up, 2, axis=-1)
    return (jax.nn.silu(g) * u) @ w_down


def alibi_slopes(n):
    return 2.0 ** (-8.0 * jnp.arange(1, n + 1, dtype=jnp.float32) / n)


def rope_tables(seq, dim):
    inv = ROPE_THETA ** (-jnp.arange(0, dim, 2, dtype=jnp.float32) / dim)
    ang = jnp.arange(seq, dtype=jnp.float32)[:, None] * inv[None, :]
    return jnp.cos(ang), jnp.sin(ang)


def apply_rope(x, cos, sin):
    half = x.shape[-1] // 2
    x1 = x[..., :half].astype(jnp.float32)
    x2 = x[..., half:].astype(jnp.float32)
    return jnp.concatenate([x1 * cos - x2 * sin, x1 * sin + x2 * cos], axis=-1).astype(x.dtype)


def swa_sink_attention(q, k, v, sinks):
    B, S = q.shape[0], q.shape[1]
    nb = S // WINDOW
    qb = q.reshape(B, nb, WINDOW, A_KV_HEADS, A_GROUP, A_HEAD_DIM)
    pad = jnp.zeros((B, WINDOW, A_KV_HEADS, A_HEAD_DIM), k.dtype)
    kp = jnp.concatenate([pad, k], axis=1).reshape(B, nb + 1, WINDOW, A_KV_HEADS, A_HEAD_DIM)
    vp = jnp.concatenate([pad, v], axis=1).reshape(B, nb + 1, WINDOW, A_KV_HEADS, A_HEAD_DIM)
    kb = jnp.concatenate([kp[:, :-1], kp[:, 1:]], axis=2)
    vb = jnp.concatenate([vp[:, :-1], vp[:, 1:]], axis=2)
    s = jnp.einsum('bnqkgd,bnskd->bnkgqs', qb, kb).astype(jnp.float32) * (A_HEAD_DIM ** -0.5)
    qi = jnp.arange(WINDOW)[:, None]
    kj = jnp.arange(2 * WINDOW)[None, :]
    dist = qi + WINDOW - kj
    band = (dist >= 0) & (dist < WINDOW)
    start_ok = (jnp.arange(nb)[:, None, None] * WINDOW + kj[None] - WINDOW) >= 0
    mask = band[None] & start_ok
    slopes = alibi_slopes(A_HEADS).reshape(A_KV_HEADS, A_GROUP)
    s = s - slopes[None, None, :, :, None, None] * dist.astype(jnp.float32)[None, None, None, None]
    s = jnp.where(mask[None, :, None, None], s, -jnp.inf)
    sink = sinks.astype(jnp.float32).reshape(A_KV_HEADS, A_GROUP)[None, None, :, :, None, None]
    m = jnp.maximum(jnp.max(s, axis=-1, keepdims=True), sink)
    e = jnp.exp(s - m)
    pr = e / (jnp.sum(e, axis=-1, keepdims=True) + jnp.exp(sink - m))
    out = jnp.einsum('bnkgqs,bnskd->bnqkgd', pr.astype(v.dtype), vb)
    return out.reshape(B, S, A_HEADS * A_HEAD_DIM)


def mla_attention(q_nope, q_rope, k_nope, k_rope, v):
    B, S = q_nope.shape[0], q_nope.shape[1]
    nb = S // QBLK
    qn = q_nope.reshape(B, nb, QBLK, B_HEADS, B_NOPE_DIM).transpose(1, 0, 2, 3, 4)
    qr = q_rope.reshape(B, nb, QBLK, B_HEADS, B_ROPE_DIM).transpose(1, 0, 2, 3, 4)
    kpos = jnp.arange(S)
    scale = (B_NOPE_DIM + B_ROPE_DIM) ** -0.5

    def one_block(args):
        qn_b, qr_b, n = args
        s = jnp.einsum('bqhd,bkhd->bhqk', qn_b, k_nope) + jnp.einsum('bqhd,bkd->bhqk', qr_b, k_rope)
        s = s.astype(jnp.float32) * scale
        qpos = n * QBLK + jnp.arange(QBLK)
        s = jnp.where(kpos[None, :] <= qpos[:, None], s, -jnp.inf)
        pr = jax.nn.softmax(s, axis=-1)
        return jnp.einsum('bhqk,bkhd->bqhd', pr.astype(v.dtype), v)

    out = lax.map(one_block, (qn, qr, jnp.arange(nb)))
    return out.transpose(1, 0, 2, 3, 4).reshape(B, S, B_HEADS * B_V_DIM)


def fox_attention(q, k, v, logc):
    B, S = q.shape[0], q.shape[1]
    nb = S // QBLK
    qb = q.reshape(B, nb, QBLK, C_HEADS, C_HEAD_DIM).transpose(1, 0, 2, 3, 4)
    cb = logc.reshape(B, nb, QBLK, C_HEADS).transpose(1, 0, 2, 3)
    c_keys = logc.transpose(0, 2, 1)
    kpos = jnp.arange(S)

    def one_block(args):
        q_b, c_b, n = args
        s = jnp.einsum('bqhd,bkhd->bhqk', q_b, k).astype(jnp.float32) * (C_HEAD_DIM ** -0.5)
        s = s + c_b.transpose(0, 2, 1)[..., None] - c_keys[:, :, None, :]
        qpos = n * QBLK + jnp.arange(QBLK)
        s = jnp.where(kpos[None, :] <= qpos[:, None], s, -jnp.inf)
        pr = jax.nn.softmax(s, axis=-1)
        return jnp.einsum('bhqk,bkhd->bqhd', pr.astype(v.dtype), v)

    out = lax.map(one_block, (qb, cb, jnp.arange(nb)))
    return out.transpose(1, 0, 2, 3, 4).reshape(B, S, C_HEADS * C_HEAD_DIM)


def even_mixer(h, w_in, sinks, cq_norm, w_uq, ckv_norm, w_ukv, w_out):
    B, S = h.shape[0], h.shape[1]
    idx = [int(i) for i in np.cumsum(EVEN_IN_SPLITS)[:-1]]
    a_q, a_k, a_v, c_q, c_kv, k_rope = jnp.split(h @ w_in, idx, axis=-1)
    out_a = swa_sink_attention(a_q.reshape(B, S, A_HEADS, A_HEAD_DIM),
                               a_k.reshape(B, S, A_KV_HEADS, A_HEAD_DIM),
                               a_v.reshape(B, S, A_KV_HEADS, A_HEAD_DIM), sinks)
    q = (rms_norm(c_q, cq_norm) @ w_uq).reshape(B, S, B_HEADS, B_NOPE_DIM + B_ROPE_DIM)
    kv = (rms_norm(c_kv, ckv_norm) @ w_ukv).reshape(B, S, B_HEADS, B_NOPE_DIM + B_V_DIM)
    cos, sin = rope_tables(S, B_ROPE_DIM)
    q_nope = q[..., :B_NOPE_DIM]
    q_rope = apply_rope(q[..., B_NOPE_DIM:], cos[None, :, None], sin[None, :, None])
    k_nope = kv[..., :B_NOPE_DIM]
    v = kv[..., B_NOPE_DIM:]
    k_rope = apply_rope(k_rope, cos[None], sin[None])
    out_b = mla_attention(q_nope, q_rope, k_nope, k_rope, v)
    return jnp.concatenate([out_a, out_b], axis=-1) @ w_out


def odd_mixer(h, w_in, b_f, w_out):
    B, S = h.shape[0], h.shape[1]
    w = ODD_MIX_DIM
    q, k, v, f_logit = jnp.split(h @ w_in, [w, 2 * w, 3 * w], axis=-1)
    logf = jax.nn.log_sigmoid(f_logit.astype(jnp.float32) + b_f.astype(jnp.float32))
    logc = jnp.cumsum(logf, axis=1)
    shp = (B, S, C_HEADS, C_HEAD_DIM)
    out = fox_attention(q.reshape(shp), k.reshape(shp), v.reshape(shp), logc)
    return out @ w_out


def _normal(key, shape, scale):
    return jax.random.normal(key, shape, jnp.float32) * scale


def setup_inputs(seed: int = 0) -> dict:
    key = jax.random.key(seed)
    ks = jax.random.split(key, 23)
    D = D_MODEL
    return {
        'x': _normal(ks[0], (BATCH, SEQ, D), 1.0),
        'p': _normal(ks[1], (DEPTH, BATCH, SEQ, PLE_DIM), 1.0),
        'ffa_norm': 1.0 + _normal(ks[2], (DEPTH, D), 0.05),
        'ffa_w_gate_up': _normal(ks[3], (DEPTH, D, 2 * D_FF), D ** -0.5),
        'ffa_w_down': _normal(ks[4], (DEPTH, D_FF, D), D_FF ** -0.5),
        'mix_norm': 1.0 + _normal(ks[5], (DEPTH, D), 0.05),
        'ffb_norm': 1.0 + _normal(ks[6], (DEPTH, D), 0.05),
        'ffb_w_gate_up': _normal(ks[7], (DEPTH, D, 2 * D_FF), D ** -0.5),
        'ffb_w_down': _normal(ks[8], (DEPTH, D_FF, D), D_FF ** -0.5),
        'ple_norm': 1.0 + _normal(ks[9], (DEPTH, D), 0.05),
        'ple_w_gate': _normal(ks[10], (DEPTH, D, D), D ** -0.5),
        'ple_w_proj': _normal(ks[11], (DEPTH, PLE_DIM, D), PLE_DIM ** -0.5),
        'ev_w_in': _normal(ks[12], (N_EVEN, D, EVEN_IN_DIM), D ** -0.5),
        'ev_sinks': _normal(ks[13], (N_EVEN, A_HEADS), 0.5),
        'ev_cq_norm': 1.0 + _normal(ks[14], (N_EVEN, B_Q_LORA), 0.05),
        'ev_w_uq': _normal(ks[15], (N_EVEN, B_Q_LORA, B_HEADS * (B_NOPE_DIM + B_ROPE_DIM)), B_Q_LORA ** -0.5),
        'ev_ckv_norm': 1.0 + _normal(ks[16], (N_EVEN, B_KV_LORA), 0.05),
        'ev_w_ukv': _normal(ks[17], (N_EVEN, B_KV_LORA, B_HEADS * (B_NOPE_DIM + B_V_DIM)), B_KV_LORA ** -0.5),
        'ev_w_out': _normal(ks[18], (N_EVEN, EVEN_MIX_DIM, D), EVEN_MIX_DIM ** -0.5),
        'od_w_in': _normal(ks[19], (N_ODD, D, ODD_IN_DIM), D ** -0.5),
        'od_b_f': FORGET_BIAS_CENTER + _normal(ks[20], (N_ODD, C_HEADS), 0.5),
        'od_w_out': _normal(ks[21], (N_ODD, ODD_MIX_DIM, D), ODD_MIX_DIM ** -0.5),
        'final_norm': 1.0 + _normal(ks[22], (D,), 0.05),
    }


def reference(x, p, ffa_norm, ffa_w_gate_up, ffa_w_down, mix_norm, ffb_norm, ffb_w_gate_up, ffb_w_down,
              ple_norm, ple_w_gate, ple_w_proj, ev_w_in, ev_sinks, ev_cq_norm, ev_w_uq, ev_ckv_norm,
              ev_w_ukv, ev_w_out, od_w_in, od_b_f, od_w_out, final_norm):
    h = x
    for i in range(DEPTH):
        j = i // 2
        h = h + FFN_RES_SCALE * swiglu(rms_norm(h, ffa_norm[i]), ffa_w_gate_up[i], ffa_w_down[i])
        hn = rms_norm(h, mix_norm[i])
        if i % 2 == 0:
            h = h + even_mixer(hn, ev_w_in[j], ev_sinks[j], ev_cq_norm[j], ev_w_uq[j],
                               ev_ckv_norm[j], ev_w_ukv[j], ev_w_out[j])
        else:
            h = h + odd_mixer(hn, od_w_in[j], od_b_f[j], od_w_out[j])
        h = h + FFN_RES_SCALE * swiglu(rms_norm(h, ffb_norm[i]), ffb_w_gate_up[i], ffb_w_down[i])
        gate = jax.nn.sigmoid(rms_norm(h, ple_norm[i]) @ ple_w_gate[i])
        h = h + gate * (p[i] @ ple_w_proj[i])
    return rms_norm(h, final_norm)
```

```python
import contextlib
import numpy as np
import ml_dtypes
import concourse.bass as bass
import concourse.mybir as mybir
from concourse.bass_utils import run_bass_kernel_spmd

F32 = mybir.dt.float32
BF16 = mybir.dt.bfloat16
AF = mybir.ActivationFunctionType
ALU = mybir.AluOpType

D = 1024
DFF = 2816
FC = DFF // 128
KC = D // 128
PLE = 256
RMS_EPS = 1e-6
N_CORES = 8

SAME_ENGINE_SYNC = True
N_DMA_SEMS = 24


class Buf:
    __slots__ = ("name", "w", "r", "rd")

    def __init__(self, name):
        self.name = name
        self.w = None
        self.r = {}
        self.rd = []


class Op:
    __slots__ = ("eng", "fn", "deps", "is_dma", "event", "clock", "needs_inc", "idx")


class Sched:
    ENGS = ("pe", "act", "dve", "pool", "sp")

    def __init__(self, nc, stack):
        self.nc = nc
        self.sem = {e: stack.enter_context(nc.semaphore("s_" + e)) for e in ("pe", "act", "dve", "pool")}
        self.dma_sems = [stack.enter_context(nc.semaphore("s_dma%d" % i)) for i in range(N_DMA_SEMS)]
        self.count = {e: 0 for e in ("pe", "act", "dve", "pool")}
        self.dma_count = [0] * N_DMA_SEMS
        self.dma_last = [None] * N_DMA_SEMS
        self.dma_i = 0
        self.ops = []
        self.bufs = []
        self.out_events = []

    def buf(self, name):
        b = Buf(name)
        self.bufs.append(b)
        return b

    def bufs_n(self, name, n):
        return [self.buf("%s%d" % (name, i)) for i in range(n)]

    def op(self, eng, fn, reads=(), writes=(), dma=False):
        o = Op()
        o.eng = eng
        o.fn = fn
        o.is_dma = dma
        o.event = None
        o.clock = None
        o.needs_inc = dma
        o.idx = len(self.ops)
        deps = set()
        for b in reads:
            if b.w is not None:
                deps.add(b.w)
        for b in writes:
            if b.w is not None:
                deps.add(b.w)
            for r in b.r.values():
                deps.add(r)
            for r in b.rd:
                deps.add(r)
        if dma:
            slot = self.dma_i % N_DMA_SEMS
            self.dma_i += 1
            if self.dma_last[slot] is not None:
                deps.add(self.dma_last[slot])
            self.dma_last[slot] = o.idx
            o.event = slot
        deps.discard(o.idx)
        fin = set()
        for d in deps:
            od = self.ops[d]
            if od.eng == eng and not od.is_dma:
                if eng == "pe" or eng == "sp" or not SAME_ENGINE_SYNC:
                    continue
            fin.add(d)
        o.deps = fin
        self.ops.append(o)
        for b in reads:
            if dma:
                b.rd.append(o.idx)
            else:
                b.r[eng] = o.idx
        for b in writes:
            b.w = o.idx
            b.r = {}
            b.rd = []
        return o

    def I(self, eng, meth, reads, writes, *args, **kw):
        return self.op(eng, lambda e: getattr(e, meth)(*args, **kw), reads, writes)

    def mm(self, out, lhsT, rhs, start, stop, reads, writes):
        return self.op("pe", lambda e: e.matmul(out, lhsT=lhsT, rhs=rhs, start=start, stop=stop), reads, writes)

    def dma(self, out, in_, reads=(), writes=(), eng="sp"):
        return self.op(eng, lambda e: e.dma_start(out=out, in_=in_), reads, writes, dma=True)

    def flush(self, final=False):
        nc = self.nc
        ops = self.ops
        for o in ops:
            for d in o.deps:
                ops[d].needs_inc = True
        for o in ops:
            if o.is_dma:
                slot = o.event
                self.dma_count[slot] += 16
                o.event = (self.dma_sems[slot], self.dma_count[slot], ("d", slot))
            elif o.needs_inc:
                self.count[o.eng] += 1
                o.event = (self.sem[o.eng], self.count[o.eng], ("e", o.eng))
        known = {e: {} for e in self.ENGS}
        plans = {e: [] for e in self.ENGS}
        for o in ops:
            k = known[o.eng]
            waits = []
            for d in sorted(o.deps):
                od = ops[d]
                sem, val, key = od.event
                if k.get(key, 0) < val:
                    waits.append((sem, val))
                    k[key] = val
                    for kk, vv in od.clock.items():
                        if k.get(kk, 0) < vv:
                            k[kk] = vv
            if o.event is not None:
                o.clock = dict(k)
            plans[o.eng].append((o, waits))
        final_waits = []
        for slot in range(N_DMA_SEMS):
            if self.dma_count[slot] > 0:
                final_waits.append((self.dma_sems[slot], self.dma_count[slot]))

        def emit(engname):
            def body(e):
                for o, waits in plans[engname]:
                    for sem, val in waits:
                        e.wait_ge(sem, val)
                    ins = o.fn(e)
                    if o.event is not None:
                        ins.then_inc(o.event[0], 16 if o.is_dma else 1)
                if engname == "sp":
                    for sem, val in final_waits:
                        e.wait_ge(sem, val)
            return body

        with nc.Block() as block:
            block.tensor(emit("pe"))
            block.scalar(emit("act"))
            block.vector(emit("dve"))
            block.gpsimd(emit("pool"))
            block.sync(emit("sp"))
        self.nops = getattr(self, "nops", 0) + len(ops)
        self.ops = []
        self.dma_last = [None] * N_DMA_SEMS
        for b in self.bufs:
            b.w = None
            b.r = {}
            b.rd = []
        self.bufs = []


class Ctx:
    N = 0

    def __init__(self, nc, sch):
        self.nc = nc
        self.sch = sch
        self.stack = contextlib.ExitStack()
        self.n = 0

    def sb(self, shape, dtype, name=None):
        Ctx.N += 1
        t = self.stack.enter_context(self.nc.sbuf_tensor("%s_%d" % (name or "t", Ctx.N), list(shape), dtype))
        return t

    def ps(self, shape=(128, 512), dtype=F32, name=None):
        Ctx.N += 1
        t = self.stack.enter_context(self.nc.psum_tensor("%s_%d" % (name or "p", Ctx.N), list(shape), dtype))
        return t

    def close(self):
        self.stack.close()


CAST_ENGS = ("pool", "dve", "act")


def cast_op(sch, eng, out, in_, reads, writes, scale=None):
    if eng == "act":
        if scale is None:
            return sch.op("act", lambda e: e.copy(out=out, in_=in_), reads, writes)
        return sch.op("act", lambda e: e.mul(out=out, in_=in_, mul=scale), reads, writes)
    if scale is None:
        return sch.op(eng, lambda e: e.tensor_copy(out=out, in_=in_), reads, writes)
    return sch.op(eng, lambda e: e.tensor_scalar(out=out, in0=in_, scalar1=float(scale), scalar2=None, op0=ALU.mult),
                  reads, writes)


class WRef:
    LOOK = 3

    def __init__(self):
        self.pieces = []
        self.next = 0

    def pump(self, upto):
        upto = min(upto, len(self.pieces) - 1)
        while self.next <= upto:
            p = self.pieces[self.next]
            p[3]()
            p[3] = None
            self.next += 1

    def cols(self, a, b):
        idx = [i for i, (x, y, _, _) in enumerate(self.pieces) if x < b and y > a]
        self.pump(idx[-1] + self.LOOK)
        return tuple(self.pieces[i][2] for i in idx)

    @property
    def all(self):
        self.pump(len(self.pieces) - 1)
        return tuple(p[2] for p in self.pieces)


class WLoader:
    def __init__(self, cx, sch, stage_elems=2048, nstage=4):
        self.sch = sch
        self.stage = [cx.sb([128, stage_elems], F32, "stage") for _ in range(nstage)]
        self.sbuf = sch.bufs_n("stage", nstage)
        self.stage_elems = stage_elems
        self.i = 0
        self.ce = 0

    def load(self, dst, src, kc, M, block=None, prefetch=2):
        sch = self.sch
        ref = WRef()
        w = block or max(64, (self.stage_elems // kc) // 64 * 64)
        m0 = 0
        while m0 < M:
            m1 = min(M, m0 + w)
            buf = sch.buf("wpiece")

            def emit(m0=m0, m1=m1, buf=buf):
                si = self.i % len(self.stage)
                self.i += 1
                eng = ("act", "dve")[self.ce % 2]
                self.ce += 1
                st_ap = self.stage[si][:, 0:kc * (m1 - m0)].rearrange("p (c m) -> p c m", c=kc)
                sch.dma(st_ap, src[:, :, m0:m1], writes=(self.sbuf[si],))
                cast_op(sch, eng, dst[:, :, m0:m1], st_ap, (self.sbuf[si],), (buf,))

            ref.pieces.append([m0, m1, buf, emit])
            m0 = m1
        ref.pump(prefetch - 1)
        return ref


def rmsnorm_stats(sch, h_ap, h_buf, kc, N, dim, T):
    sq, ss, rstd, ones, sd = T["sq"], T["ss"], T["rstd"], T["ones"], T["sd"]
    sch.op("act", lambda e: e.activation(out=sq[:, 0:kc, 0:N], in_=h_ap, func=AF.Square),
           reads=(h_buf,), writes=(T["sq_b"],))
    for c in range(kc):
        sch.op("pe", lambda e, c=c: e.matmul(ss[:, 0:N], lhsT=ones[:, :], rhs=sq[:, c, 0:N],
                                             start=(c == 0), stop=(c == kc - 1)),
               reads=(T["sq_b"], T["ones_b"]), writes=(T["ss_b"],))
    sch.I("act", "activation", (T["ss_b"],), (T["sd_b"],), out=sd[:, 0:N], in_=ss[:, 0:N], func=AF.Ln,
          bias=T["eps"][:, 0:1], scale=1.0 / dim)
    sch.I("act", "activation", (T["sd_b"],), (T["rstd_b"],), out=rstd[:, 0:N], in_=sd[:, 0:N], func=AF.Exp, scale=-0.5)


def norm_tiles(cx, sch, kc, N):
    T = {}
    T["sq"] = cx.sb([128, kc, N], BF16, "sq")
    T["sq_b"] = sch.buf("sq")
    T["ss"] = cx.ps([128, 512], F32, "ss")
    T["ss_b"] = sch.buf("ss")
    T["rstd"] = cx.sb([128, N], F32, "rstd")
    T["rstd_b"] = sch.buf("rstd")
    T["sd"] = cx.sb([128, N], F32, "sd")
    T["sd_b"] = sch.buf("sd")
    T["ones"] = cx.sb([128, 128], BF16, "ones")
    T["ones_b"] = sch.buf("ones")
    T["eps"] = cx.sb([128, 1], F32, "eps")
    sch.op("pool", lambda e: e.memset(T["ones"][:, :], 1.0), writes=(T["ones_b"],))
    sch.op("pool", lambda e: e.memset(T["eps"][:, :], RMS_EPS), writes=(T["sd_b"],))
    return T


def apply_norm(sch, xn, xn_buf, h_t, h_buf, g_sb, g_buf, T, kc, N, eng="dve"):
    rstd = T["rstd"]
    for c in range(kc):
        sch.op(eng, lambda e, c=c: e.scalar_tensor_tensor(out=xn[:, c, 0:N], in0=h_t[:, c, 0:N],
                                                          scalar=g_sb[:, c:c + 1], in1=rstd[:, 0:N],
                                                          op0=ALU.mult, op1=ALU.mult),
               reads=(h_buf, g_buf, T["rstd_b"]), writes=(xn_buf,))


def phase_ffn(nc, sch, S, h_in, h_out, wgu, wd, g):
    N = 256 if S >= 256 else S
    NT = S // N
    cx = Ctx(nc, sch)
    wgu_bf = cx.sb([128, FC, KC, 256], BF16, "wgu")
    wd_bf = cx.sb([128, FC, D], BF16, "wd")
    wgu_b = sch.bufs_n("wgu", FC)
    wd_b = sch.bufs_n("wd", FC // 2)
    NST = 3
    stage = [cx.sb([128, 2048], F32, "stage") for _ in range(NST)]
    stage_b = sch.bufs_n("stage", NST)
    g_sb = cx.sb([128, KC], F32, "g")
    g_b = sch.buf("g")
    T = norm_tiles(cx, sch, KC, N)
    NH = 3
    h_t = [cx.sb([128, KC, N], F32, "h") for _ in range(NH)]
    h_b = sch.bufs_n("h", NH)
    xn = [cx.sb([128, KC, N], BF16, "xn") for _ in range(2)]
    xn_b = sch.bufs_n("xn", 2)
    act = cx.sb([128, FC, N], BF16, "act")
    act_b = sch.bufs_n("act", FC)
    sil = [cx.sb([128, N], F32, "sil") for _ in range(2)]
    sil_b = sch.bufs_n("sil", 2)
    pg = [cx.ps() for _ in range(2)]
    pu = [cx.ps() for _ in range(2)]
    po = [cx.ps() for _ in range(2)]
    pg_b, pu_b, po_b = sch.bufs_n("pg", 2), sch.bufs_n("pu", 2), sch.bufs_n("po", 2)
    hv = h_in.rearrange("(c p) s -> p c s", p=128)
    ho = h_out.rearrange("(c p) s -> p c s", p=128)

    sch.dma(g_sb[:, :], g, writes=(g_b,))
    sch.dma(h_t[0][:, :, :], hv[:, :, 0:N], writes=(h_b[0],))
    engs = ("act", "dve", "act")
    pieces = []
    for f in range(FC):
        pieces.append(("gu", f))
        if f % 2 == 1:
            pieces.append(("d", f // 2))
    pstate = {"i": 0}

    def emit_piece():
        if pstate["i"] >= len(pieces):
            return
        kind, ix = pieces[pstate["i"]]
        k = pstate["i"] % NST
        eng = engs[pstate["i"] % 3]
        pstate["i"] += 1
        if kind == "gu":
            sch.dma(stage[k][:, :], wgu[:, ix, :, :].rearrange("p c j -> p (c j)"), writes=(stage_b[k],))
            cast_op(sch, eng, wgu_bf[:, ix, :, :].rearrange("p c j -> p (c j)"), stage[k][:, :], (stage_b[k],), (wgu_b[ix],))
        else:
            sch.dma(stage[k][:, :], wd[:, 2 * ix:2 * ix + 2, :].rearrange("p f d -> p (f d)"), writes=(stage_b[k],))
            cast_op(sch, eng, wd_bf[:, 2 * ix:2 * ix + 2, :].rearrange("p f d -> p (f d)"), stage[k][:, :], (stage_b[k],),
                    (wd_b[ix],))

    def norm(t):
        rmsnorm_stats(sch, h_t[t % NH][:, :, :], h_b[t % NH], KC, N, D, T)
        apply_norm(sch, xn[t % 2], xn_b[t % 2], h_t[t % NH], h_b[t % NH], g_sb, g_b, T, KC, N)

    norm(0)
    if NT > 1:
        sch.dma(h_t[1][:, :, :], hv[:, :, N:2 * N], writes=(h_b[1],))
    for _ in range(NST):
        emit_piece()
    for t in range(NT):
        i = t % 2
        ih = t % NH
        if t + 2 < NT:
            sch.dma(h_t[(t + 2) % NH][:, :, :], hv[:, :, (t + 2) * N:(t + 3) * N], writes=(h_b[(t + 2) % NH],))
        for f in range(FC):
            j = f % 2
            if t == 0:
                emit_piece()
                emit_piece()
            for c in range(KC):
                sch.mm(pg[j][:, 0:N], wgu_bf[:, f, c, 0:128], xn[i][:, c, :], c == 0, c == KC - 1,
                       (wgu_b[f], xn_b[i]), (pg_b[j],))
            for c in range(KC):
                sch.mm(pu[j][:, 0:N], wgu_bf[:, f, c, 128:256], xn[i][:, c, :], c == 0, c == KC - 1,
                       (wgu_b[f], xn_b[i]), (pu_b[j],))
            sch.I("act", "activation", (pg_b[j],), (sil_b[j],), out=sil[j][:, :], in_=pg[j][:, 0:N], func=AF.Silu)
            sch.I("dve", "tensor_tensor", (pu_b[j], sil_b[j]), (act_b[f],), out=act[:, f, :], in0=pu[j][:, 0:N],
                  in1=sil[j][:, :], op=ALU.mult)
            if t + 1 < NT:
                i2 = (t + 1) % 2
                ih2 = (t + 1) % NH
                if f == 5:
                    sch.I("act", "activation", (h_b[ih2],), (T["sq_b"],), out=T["sq"][:, 0:KC, 0:N], in_=h_t[ih2][:, :, :],
                          func=AF.Square)
                elif f == 9:
                    for c in range(KC):
                        sch.mm(T["ss"][:, 0:N], T["ones"][:, :], T["sq"][:, c, 0:N], c == 0, c == KC - 1,
                               (T["sq_b"], T["ones_b"]), (T["ss_b"],))
                    sch.I("act", "activation", (T["ss_b"],), (T["sd_b"],), out=T["sd"][:, 0:N], in_=T["ss"][:, 0:N],
                          func=AF.Ln, bias=T["eps"][:, 0:1], scale=1.0 / D)
                    sch.I("act", "activation", (T["sd_b"],), (T["rstd_b"],), out=T["rstd"][:, 0:N], in_=T["sd"][:, 0:N],
                          func=AF.Exp, scale=-0.5)
                elif 11 <= f < 11 + KC:
                    c = f - 11
                    sch.I("dve", "scalar_tensor_tensor", (h_b[ih2], g_b, T["rstd_b"]), (xn_b[i2],), out=xn[i2][:, c, 0:N],
                          in0=h_t[ih2][:, c, 0:N], scalar=g_sb[:, c:c + 1], in1=T["rstd"][:, 0:N], op0=ALU.mult, op1=ALU.mult)
        for c in range(KC):
            j = c % 2
            for f in range(FC):
                sch.mm(po[j][:, 0:N], wd_bf[:, f, c * 128:(c + 1) * 128], act[:, f, :], f == 0, f == FC - 1,
                       (wd_b[f // 2], act_b[f]), (po_b[j],))
            sch.I("dve", "scalar_tensor_tensor", (po_b[j], h_b[ih]), (h_b[ih],), out=h_t[ih][:, c, :], in0=po[j][:, 0:N],
                  scalar=0.5, in1=h_t[ih][:, c, :], op0=ALU.mult, op1=ALU.add)
        sch.dma(ho[:, :, t * N:(t + 1) * N], h_t[ih][:, :, :], reads=(h_b[ih],))
    sch.flush()
    cx.close()


class PsRing:
    def __init__(self, cx, sch, n, name="pp", shape=(128, 512)):
        self.t = [cx.ps(shape) for _ in range(n)]
        self.b = sch.bufs_n(name, n)
        self.i = 0

    def next(self):
        k = self.i % len(self.t)
        self.i += 1
        return self.t[k], self.b[k]


class EvacRR:
    def __init__(self, sch, pattern=("act", "dve")):
        self.sch = sch
        self.i = 0
        self.pattern = pattern

    def copy(self, out, in_, reads, writes, scale=None, eng=None):
        if eng is None:
            eng = self.pattern[self.i % len(self.pattern)]
            self.i += 1
        return cast_op(self.sch, eng, out, in_, reads, writes, scale)


def phase_proj_even(nc, sch, S, h_in, A):
    N = 512 if S >= 512 else S
    NT = S // N
    NS = N // 128
    cx = Ctx(nc, sch)
    W_IN = 1184
    win = cx.sb([128, KC, W_IN], BF16, "win")
    win_b = sch.buf("win")
    wkr = cx.sb([128, KC, 96], BF16, "wkr")
    wkrr = cx.sb([128, KC, 96], BF16, "wkrr")
    wkr_b, wkrr_b = sch.buf("wkr"), sch.buf("wkrr")
    wuq = cx.sb([128, 2, 768], BF16, "wuq")
    wuqr = cx.sb([128, 2, 8, 96], BF16, "wuqr")
    wuq_b, wuqr_b = sch.buf("wuq"), sch.buf("wuqr")
    wukv = cx.sb([128, 1, 1024], BF16, "wukv")
    wukv_b = sch.buf("wukv")
    wkn = cx.sb([128, 8, 64], BF16, "wkn")
    wvm = cx.sb([128, 8, 64], BF16, "wvm")
    wkn_b, wvm_b = sch.buf("wkn"), sch.buf("wvm")
    g_sb = cx.sb([128, KC], F32, "g")
    gq_sb = cx.sb([128, 2], F32, "gq")
    gkv_sb = cx.sb([128, 1], F32, "gkv")
    g_b = sch.buf("gs")
    T = norm_tiles(cx, sch, KC, N)
    h_t = [cx.sb([128, KC, N], F32, "h") for _ in range(2)]
    h_b = sch.bufs_n("h", 2)
    xn2 = [cx.sb([128, KC, N], BF16, "xn") for _ in range(2)]
    xn2_b = sch.bufs_n("xn", 2)
    cs = [cx.sb([96, 2, N], F32, "cs") for _ in range(2)]
    cs_b = sch.bufs_n("cs", 2)
    pr = PsRing(cx, sch, 6)
    ev = EvacRR(sch, pattern=("act", "act", "act", "dve"))
    NOUT = 6
    ob = [cx.sb([128, N], BF16, "ob") for _ in range(NOUT)]
    ob_b = sch.bufs_n("ob", NOUT)
    oi = [0]

    def nxt_ob():
        k = oi[0] % NOUT
        oi[0] += 1
        return ob[k], ob_b[k]

    cq = cx.sb([128, 2, N], F32, "cq")
    cq_b = sch.buf("cq")
    cqn = cx.sb([128, 2, N], BF16, "cqn")
    cqn_b = sch.buf("cqn")
    ckv = cx.sb([128, 1, N], F32, "ckv")
    ckv_b = sch.buf("ckv")
    ckvn = cx.sb([128, 1, N], BF16, "ckvn")
    ckvn_b = sch.buf("ckvn")
    t1 = [cx.sb([96, N], F32, "t1") for _ in range(2)]
    t2 = [cx.sb([96, N], F32, "t2") for _ in range(2)]
    t1_b, t2_b = sch.bufs_n("t1", 2), sch.bufs_n("t2", 2)
    hv = h_in.rearrange("(c p) s -> p c s", p=128)

    sch.dma(g_sb[:, :], A["mix_g0"], writes=(g_b,))
    sch.dma(gq_sb[:, :], A["ev_cq_g"], writes=(g_b,))
    sch.dma(gkv_sb[:, :], A["ev_ckv_g"], writes=(g_b,))
    sch.dma(h_t[0][:, :, :], hv[:, :, 0:N], writes=(h_b[0],))
    sch.dma(cs[0][64:96, :, :], A["rope_cs"][:, :, 0:N], writes=(cs_b[0],))
    def norm(t):
        i_ = t % 2
        rmsnorm_stats(sch, h_t[i_][:, :, :], h_b[i_], KC, N, D, T)
        apply_norm(sch, xn2[i_], xn2_b[i_], h_t[i_], h_b[i_], g_sb, g_b, T, KC, N)

    norm(0)
    sch.I("pool", "memset", (), (wkr_b,), wkr[:, :, :], 0.0)
    sch.I("pool", "memset", (), (wkrr_b,), wkrr[:, :, :], 0.0)
    sch.I("pool", "memset", (), (wuqr_b,), wuqr[:, :, :, :], 0.0)
    wl = WLoader(cx, sch, stage_elems=2048)
    win_r = wl.load(win, A["ev_w_in"], KC, W_IN)
    wuq_r = wl.load(wuq, A["ev_w_uq"], 2, 768, prefetch=0)
    wukv_r = wl.load(wukv, A["ev_w_ukv"], 1, 1024, prefetch=0)
    wuq4 = wuq[:, :, :].rearrange("p c (h e) -> p c h e", e=96)
    wukv4 = wukv[:, 0, :].rearrange("p (h e) -> p h e", e=128)

    def derive_uq():
        sch.I("dve", "tensor_scalar", wuq_r.all, (wuqr_b,), out=wuqr[:, :, :, 64:80], in0=wuq4[:, :, :, 80:96], scalar1=-1.0,
              scalar2=None, op0=ALU.mult)
        sch.I("act", "copy", wuq_r.all, (wuqr_b,), out=wuqr[:, :, :, 80:96], in_=wuq4[:, :, :, 64:80])

    def derive_kv():
        sch.I("dve", "tensor_copy", wukv_r.all, (wkn_b,), out=wkn[:, :, :], in_=wukv4[:, :, 0:64])
        sch.I("act", "copy", wukv_r.all, (wvm_b,), out=wvm[:, :, :], in_=wukv4[:, :, 64:128])

    def derive_kr():
        kr_c = win_r.cols(1152, 1184)
        sch.I("act", "copy", kr_c, (wkr_b,), out=wkr[:, :, 64:96], in_=win[:, :, 1152:1184])
        sch.I("dve", "tensor_scalar", kr_c, (wkrr_b,), out=wkrr[:, :, 64:80], in0=win[:, :, 1168:1184], scalar1=-1.0,
              scalar2=None, op0=ALU.mult)
        sch.I("act", "copy", kr_c, (wkrr_b,), out=wkrr[:, :, 80:96], in_=win[:, :, 1152:1168])

    QS, KS, VS, QM, KM, KR, VM = A["QS"], A["KS"], A["VS"], A["QM"], A["KM"], A["KR"], A["VM"]
    QSv = QS.rearrange("g d (n h q) -> g d n h q", h=4, q=128)
    VSv = VS.rearrange("(n p) f -> p n f", p=128)
    VMv = VM.rearrange("(n p) f -> p n f", p=128)

    def small_norm(src, src_b, kc, dim, g_ap, dst, dst_b):
        rmsnorm_stats(sch, src[:, 0:kc, :], src_b, kc, N, dim, T)
        for c in range(kc):
            sch.I("dve", "scalar_tensor_tensor", (src_b, g_b, T["rstd_b"]), (dst_b,), out=dst[:, c, :], in0=src[:, c, :],
                  scalar=g_ap[:, c:c + 1], in1=T["rstd"][:, 0:N], op0=ALU.mult, op1=ALU.mult)

    cur = {}

    def lin(col0, m, w=None, wb=None, kc=KC, rhs=None, rhs_b=None):
        if w is None:
            w, wb = win, win_r.cols(col0, col0 + m)
        rhs = cur["xn"] if rhs is None else rhs
        rhs_b = cur["xn_b"] if rhs_b is None else rhs_b
        p, pb = pr.next()
        for c in range(kc):
            sch.mm(p[0:m, 0:N], w[:, c, col0:col0 + m], rhs[:, c, :], c == 0, c == kc - 1, tuple(wb) + (rhs_b,), (pb,))
        return p, pb

    for t in range(NT):
        i = t % 2
        tok = slice(t * N, (t + 1) * N)
        if t + 1 < NT:
            sch.dma(h_t[1 - i][:, :, :], hv[:, :, (t + 1) * N:(t + 2) * N], writes=(h_b[1 - i],))
            sch.dma(cs[1 - i][64:96, :, :], A["rope_cs"][:, :, (t + 1) * N:(t + 2) * N], writes=(cs_b[1 - i],))
        xn, xn_b = xn2[i], xn2_b[i]
        cur["xn"], cur["xn_b"] = xn, xn_b
        for c2 in range(2):
            p, pb = lin(768 + c2 * 128, 128)
            ev.copy(cq[:, c2, :], p[:, 0:N], (pb,), (cq_b,))
        p, pb = lin(1024, 128)
        ev.copy(ckv[:, 0, :], p[:, 0:N], (pb,), (ckv_b,))

        def swa_q(f):
            p, pb = lin(f * 128, 128)
            o, o_b = nxt_ob()
            ev.copy(o[:, :], p[:, 0:N], (pb,), (o_b,), scale=0.125)
            for hh2 in range(2):
                head = 2 * f + hh2
                g_, hh = head // 4, head % 4
                sch.dma(QSv[g_, :, t * NS:(t + 1) * NS, hh, :], o[hh2 * 64:(hh2 + 1) * 64, :].rearrange("p (n q) -> p n q", q=128),
                        reads=(o_b,))

        swa_q(0)
        swa_q(1)
        small_norm(cq, cq_b, 2, 256, gq_sb, cqn, cqn_b)
        swa_q(2)
        swa_q(3)
        small_norm(ckv, ckv_b, 1, 128, gkv_sb, ckvn, ckvn_b)
        p, pb = lin(512, 128)
        o, o_b = nxt_ob()
        ev.copy(o[:, :], p[:, 0:N], (pb,), (o_b,))
        sch.dma(KS[:, tok], o[:, :], reads=(o_b,))
        p, pb = pr.next()
        for s_ in range(NS):
            for c in range(KC):
                sch.mm(p[:, s_ * 128:(s_ + 1) * 128], xn[:, c, s_ * 128:(s_ + 1) * 128], win[:, c, 640:768], c == 0,
                       c == KC - 1, win_r.cols(640, 768) + (xn_b,), (pb,))
        o, o_b = nxt_ob()
        ev.copy(o[:, 0:NS * 128], p[:, 0:NS * 128], (pb,), (o_b,))
        sch.dma(VSv[:, t * NS:(t + 1) * NS, :], o[:, 0:NS * 128].rearrange("p (n f) -> p n f", f=128), reads=(o_b,))

        def rope_rows(pq, pq_b, prr, prr_b, o, o_b):
            k = oi[0] % 2
            sch.I("dve", "tensor_tensor", (pq_b, cs_b[i]), (t1_b[k],), out=t1[k][64:96, :], in0=pq[64:96, 0:N],
                  in1=cs[i][64:96, 0, :], op=ALU.mult)
            sch.I("dve", "tensor_tensor", (prr_b, cs_b[i]), (t2_b[k],), out=t2[k][64:96, :], in0=prr[64:96, 0:N],
                  in1=cs[i][64:96, 1, :], op=ALU.mult)
            sch.I("dve", "tensor_tensor", (t1_b[k], t2_b[k]), (o_b,), out=o[64:96, :], in0=t1[k][64:96, :],
                  in1=t2[k][64:96, :], op=ALU.add)

        if t == 0:
            derive_uq()
        for hd in range(8):
            pq, pq_b = lin(hd * 96, 96, w=wuq, wb=wuq_r.all, kc=2, rhs=cqn, rhs_b=cqn_b)
            prr, prr_b = pr.next()
            for c in range(2):
                sch.mm(prr[0:96, 0:N], wuqr[:, c, hd, :], cqn[:, c, :], c == 0, c == 1, (wuqr_b, cqn_b), (prr_b,))
            o, o_b = nxt_ob()
            sch.I("act", "copy", (pq_b,), (o_b,), out=o[0:64, :], in_=pq[0:64, 0:N])
            rope_rows(pq, pq_b, prr, prr_b, o, o_b)
            sch.dma(QM[hd, :, tok], o[0:96, :], reads=(o_b,))
        if t + 1 < NT:
            norm(t + 1)
        if t == 0:
            derive_kv()
        for hp in range(4):
            p, pb = pr.next()
            sch.mm(p[:, 0:N], wkn[:, 2 * hp:2 * hp + 2, :], ckvn[:, 0, :], True, True, (wkn_b, ckvn_b), (pb,))
            o, o_b = nxt_ob()
            ev.copy(o[:, :], p[:, 0:N], (pb,), (o_b,))
            sch.dma(KM[2 * hp, :, tok], o[0:64, :], reads=(o_b,))
            sch.dma(KM[2 * hp + 1, :, tok], o[64:128, :], reads=(o_b,))
        for s_ in range(NS):
            p, pb = pr.next()
            sch.mm(p[:, 0:512], ckvn[:, 0, s_ * 128:(s_ + 1) * 128], wvm[:, :, :], True, True, (wvm_b, ckvn_b), (pb,))
            o, o_b = nxt_ob()
            ev.copy(o[:, 0:512], p[:, 0:512], (pb,), (o_b,))
            sch.dma(VMv[:, t * NS + s_, :], o[:, 0:512], reads=(o_b,))
        if t == 0:
            derive_kr()
        pq, pq_b = lin(0, 96, w=wkr, wb=(wkr_b,))
        prr, prr_b = lin(0, 96, w=wkrr, wb=(wkrr_b,))
        o, o_b = nxt_ob()
        rope_rows(pq, pq_b, prr, prr_b, o, o_b)
        sch.dma(KR[:, tok], o[64:96, :], reads=(o_b,))
    sch.flush()
    cx.close()


def phase_proj_odd(nc, sch, S, h_in, A):
    N = 512 if S >= 512 else S
    NT = S // N
    NS = N // 128
    cx = Ctx(nc, sch)
    W_IN = 3088
    win = cx.sb([128, KC, W_IN], BF16, "win")
    win_b = sch.buf("win")
    g_sb = cx.sb([128, KC], F32, "g")
    g_b = sch.buf("gs")
    nbf = cx.sb([16, 1], F32, "nbf")
    nbf_b = sch.buf("nbf")
    T = norm_tiles(cx, sch, KC, N)
    h_t = [cx.sb([128, KC, N], F32, "h") for _ in range(2)]
    h_b = sch.bufs_n("h", 2)
    xn2 = [cx.sb([128, KC, N], BF16, "xn") for _ in range(2)]
    xn2_b = sch.bufs_n("xn", 2)
    pr = PsRing(cx, sch, 6)
    ev = EvacRR(sch)
    NOUT = 6
    ob = [cx.sb([128, N], BF16, "ob") for _ in range(NOUT)]
    ob_b = sch.bufs_n("ob", NOUT)
    oi = [0]

    def nxt_ob():
        k = oi[0] % NOUT
        oi[0] += 1
        return ob[k], ob_b[k]

    ones16 = cx.sb([16, N], F32, "ones16")
    ones3 = cx.sb([16, 3, N], BF16, "ones3")
    on_b = sch.buf("ones16")
    e1 = cx.sb([16, N], F32, "e1")
    lf = cx.sb([16, N], F32, "lf")
    e1_b, lf_b = sch.buf("e1"), sch.buf("lf")
    cc = [cx.sb([16, N], F32, "cc") for _ in range(2)]
    cc_b = sch.bufs_n("cc", 2)
    pc = [cx.sb([16, 3, N], BF16, "pc") for _ in range(2)]
    nc_ = [cx.sb([16, 3, N], BF16, "ncs") for _ in range(2)]
    pc_b, nc_b = sch.bufs_n("pc", 2), sch.bufs_n("ncs", 2)
    r32 = cx.sb([16, N], F32, "r32")
    r1 = cx.sb([16, N], F32, "r1")
    r2 = cx.sb([16, N], F32, "r2")
    r_b = sch.buf("r")
    hv = h_in.rearrange("(c p) s -> p c s", p=128)
    QF, KF, VF = A["QF"], A["KF"], A["VF"]
    VFv = VF.rearrange("(n p) f -> p n f", p=128)

    sch.dma(g_sb[:, :], A["mix_g1"], writes=(g_b,))
    sch.dma(nbf[:, :], A["od_bf"], writes=(nbf_b,))
    sch.I("dve", "tensor_scalar", (nbf_b,), (nbf_b,), out=nbf[:, :], in0=nbf[:, :], scalar1=-1.0, scalar2=None, op0=ALU.mult)
    sch.I("pool", "memset", (), (on_b,), ones16[:, :], 1.0)
    sch.I("pool", "memset", (), (on_b,), ones3[:, :, :], 1.0)
    sch.dma(h_t[0][:, :, :], hv[:, :, 0:N], writes=(h_b[0],))
    def norm(t):
        i_ = t % 2
        rmsnorm_stats(sch, h_t[i_][:, :, :], h_b[i_], KC, N, D, T)
        apply_norm(sch, xn2[i_], xn2_b[i_], h_t[i_], h_b[i_], g_sb, g_b, T, KC, N)

    norm(0)
    wl = WLoader(cx, sch, stage_elems=2048)
    win_r = wl.load(win[:, :, 3072:3088], A["od_w_in"][:, :, 3072:3088], KC, 16)
    win_r2 = wl.load(win[:, :, 0:3072], A["od_w_in"][:, :, 0:3072], KC, 3072)
    cur = {}

    def wcols(a, b_):
        if a >= 3072:
            return win_r.all
        return win_r2.cols(a, b_)

    def lin(col0, m):
        p, pb = pr.next()
        xn, xn_b = cur["xn"], cur["xn_b"]
        for c in range(KC):
            sch.mm(p[0:m, 0:N], win[:, c, col0:col0 + m], xn[:, c, :], c == 0, c == KC - 1, wcols(col0, col0 + m) + (xn_b,),
                   (pb,))
        return p, pb

    for t in range(NT):
        i = t % 2
        tok = slice(t * N, (t + 1) * N)
        if t + 1 < NT:
            sch.dma(h_t[1 - i][:, :, :], hv[:, :, (t + 1) * N:(t + 2) * N], writes=(h_b[1 - i],))
        xn, xn_b = xn2[i], xn2_b[i]
        cur["xn"], cur["xn_b"] = xn, xn_b
        p, pb = lin(3072, 16)
        sch.I("act", "activation", (pb, nbf_b), (e1_b,), out=e1[:, :], in_=p[0:16, 0:N], func=AF.Exp, bias=nbf[:, 0:1], scale=-1.0)
        sch.I("act", "activation", (e1_b,), (lf_b,), out=lf[:, :], in_=e1[:, :], func=AF.Ln, bias=1.0, scale=1.0)
        init = 0.0 if t == 0 else cc[1 - i][:, N - 1:N]
        sch.I("dve", "tensor_tensor_scan", (lf_b, on_b, cc_b[1 - i]), (cc_b[i],), out=cc[i][:, :], data0=ones16[:, :],
              data1=lf[:, :], initial=init, op0=ALU.mult, op1=ALU.subtract)
        P_, Nn = pc[i], nc_[i]
        sch.I("dve", "tensor_copy", (cc_b[i],), (pc_b[i],), out=P_[:, 0, :], in_=cc[i][:, :])
        sch.I("dve", "tensor_copy", (pc_b[i],), (r_b,), out=r32[:, :], in_=P_[:, 0, :])
        sch.I("dve", "tensor_tensor", (cc_b[i], r_b), (r_b,), out=r1[:, :], in0=cc[i][:, :], in1=r32[:, :], op=ALU.subtract)
        sch.I("dve", "tensor_copy", (r_b,), (pc_b[i],), out=P_[:, 1, :], in_=r1[:, :])
        sch.I("dve", "tensor_copy", (pc_b[i],), (r_b,), out=r32[:, :], in_=P_[:, 1, :])
        sch.I("dve", "tensor_tensor", (r_b,), (r_b,), out=r2[:, :], in0=r1[:, :], in1=r32[:, :], op=ALU.subtract)
        sch.I("dve", "tensor_copy", (r_b,), (pc_b[i],), out=P_[:, 2, :], in_=r2[:, :])
        sch.I("act", "mul", (pc_b[i],), (nc_b[i],), out=Nn[:, :, :], in_=P_[:, :, :], mul=-1.0)
        sch.dma(QF[:, 64:67, tok], P_[:, :, :], reads=(pc_b[i],))
        sch.dma(KF[:, 67:70, tok], Nn[:, :, :], reads=(nc_b[i],))
        sch.dma(QF[:, 67:70, tok], ones3[:, :, :], reads=(on_b,))
        sch.dma(KF[:, 64:67, tok], ones3[:, :, :], reads=(on_b,))
        for f in range(8):
            p, pb = lin(f * 128, 128)
            o, o_b = nxt_ob()
            ev.copy(o[:, :], p[:, 0:N], (pb,), (o_b,), scale=0.125)
            sch.dma(QF[2 * f, 0:64, tok], o[0:64, :], reads=(o_b,))
            sch.dma(QF[2 * f + 1, 0:64, tok], o[64:128, :], reads=(o_b,))
        if t + 1 < NT:
            norm(t + 1)
        for f in range(8):
            p, pb = lin(1024 + f * 128, 128)
            o, o_b = nxt_ob()
            ev.copy(o[:, :], p[:, 0:N], (pb,), (o_b,))
            sch.dma(KF[2 * f, 0:64, tok], o[0:64, :], reads=(o_b,))
            sch.dma(KF[2 * f + 1, 0:64, tok], o[64:128, :], reads=(o_b,))
        for s_ in range(NS):
            for half in range(2):
                p, pb = pr.next()
                for c in range(KC):
                    sch.mm(p[:, 0:512], xn[:, c, s_ * 128:(s_ + 1) * 128], win[:, c, 2048 + half * 512:2048 + (half + 1) * 512],
                           c == 0, c == KC - 1, wcols(2048 + half * 512, 2048 + (half + 1) * 512) + (xn_b,), (pb,))
                o, o_b = nxt_ob()
                ev.copy(o[:, 0:512], p[:, 0:512], (pb,), (o_b,))
                sch.dma(VFv[:, t * NS + s_, half * 512:(half + 1) * 512], o[:, 0:512], reads=(o_b,))
    sch.flush()
    cx.close()


def phase_attn_swa(nc, sch, S, A):
    NB = S // 128
    cx = Ctx(nc, sch)
    QS, KS, VS, O = A["QS"], A["KS"], A["VS"], A["O"]
    q_t = [cx.sb([66, S * 4], BF16, "q") for _ in range(2)]
    k_c = [cx.sb([66, S], BF16, "kc") for _ in range(2)]
    k_p = [cx.sb([66, S], BF16, "kp") for _ in range(2)]
    v_t = cx.sb([128, NB, 2, 128], BF16, "v")
    mask = cx.sb([128, 2, 512], BF16, "mask")
    negi = cx.sb([128, 128], BF16, "negi")
    qkv_b = sch.buf("qkv")
    es = cx.sb([1, 8], F32, "es")
    esr = cx.sb([1, 2, 512], BF16, "esr")
    zer = cx.sb([1, 128], F32, "zer")
    selz = cx.sb([1, 128], BF16, "selz")
    c_b = sch.buf("const")
    ps_s = PsRing(cx, sch, 4, "ps_s")
    ps_o = PsRing(cx, sch, 3, "ps_o")
    NSB = 4
    pt = [cx.sb([128, 512], BF16, "pt") for _ in range(NSB)]
    pt_b = sch.bufs_n("pt", NSB)
    rden = [cx.sb([64, 512], F32, "rden") for _ in range(2)]
    ot = [cx.sb([64, 512], BF16, "ot") for _ in range(2)]
    den_b, ot_b = sch.bufs_n("den", 2), sch.bufs_n("ot", 2)

    sch.I("pool", "memset", (), (qkv_b,), v_t[:, :, :, 64:128], 1.0)
    for g_ in range(2):
        sch.dma(q_t[g_][0:64, :], QS[g_, :, :], writes=(qkv_b,))
        sch.dma(q_t[g_][64:66, :], A["swa_qaug"][g_, :, :], writes=(qkv_b,))
        sch.dma(k_c[g_][0:64, :], KS[g_ * 64:(g_ + 1) * 64, :], writes=(qkv_b,))
        sch.dma(k_p[g_][0:64, :], KS[g_ * 64:(g_ + 1) * 64, :], writes=(qkv_b,))
        sch.dma(k_c[g_][64:66, :], A["swa_kaug"][1, :, :], writes=(qkv_b,))
        sch.dma(k_p[g_][64:66, :], A["swa_kaug"][0, :, :], writes=(qkv_b,))
    VSv = VS.rearrange("(n p) f -> p n f", p=128)
    for g_ in range(2):
        sch.dma(v_t[:, :, g_, 0:64], VSv[:, :, g_ * 64:(g_ + 1) * 64], writes=(qkv_b,))
    sch.dma(mask[:, :, :], A["swa_mask"], writes=(c_b,))
    sch.dma(negi[:, :], A["negi"], writes=(c_b,))
    sch.dma(es[:, :], A["ev_sinks"][0:1, :], writes=(c_b,))
    sch.I("pool", "memset", (), (c_b,), zer[:, :], 0.0)
    sch.I("pool", "memset", (), (c_b,), selz[:, 0:64], 0.0)
    sch.I("pool", "memset", (), (c_b,), selz[:, 64:128], 1.0)
    sch.I("act", "activation", (c_b,), (c_b,), out=es[:, :], in_=es[:, :], func=AF.Exp)
    for hd in range(8):
        sch.I("dve", "tensor_scalar", (c_b,), (c_b,), out=esr[:, hd // 4, (hd % 4) * 128:(hd % 4 + 1) * 128],
              in0=zer[:, :], scalar1=es[:, hd:hd + 1], scalar2=None, op0=ALU.add)
    Ov = O.rearrange("(h d) s -> d h s", d=64)
    steps = []
    for n in range(NB):
        for g_ in range(2):
            tiles = ([(0, n - 1)] if n > 0 else []) + [(1, n)]
            for ti, (typ, kt) in enumerate(tiles):
                steps.append((n, g_, typ, kt, ti == 0, ti == len(tiles) - 1))
    state = {}

    def qk(idx):
        n, g_, typ, kt, first, last = steps[idx]
        k = idx % NSB
        p, pb = ps_s.next()
        kk = k_c[g_] if typ == 1 else k_p[g_]
        sch.mm(p[:, 0:512], kk[:, kt * 128:(kt + 1) * 128], q_t[g_][:, n * 512:(n + 1) * 512], True, False,
               (qkv_b,), (pb,))
        sch.mm(p[:, 0:512], negi[:, :], mask[:, typ, :], False, True, (c_b,), (pb,))
        sch.I("act", "activation", (pb,), (pt_b[k],), out=pt[k][:, :], in_=p[:, 0:512], func=AF.Exp)

    def pv(idx):
        n, g_, typ, kt, first, last = steps[idx]
        k = idx % NSB
        if first:
            state["po"], state["po_b"] = ps_o.next()
            sch.mm(state["po"][:, 0:512], selz[:, :], esr[:, g_, :], True, False, (c_b,), (state["po_b"],))
        po, po_b = state["po"], state["po_b"]
        sch.mm(po[:, 0:512], v_t[:, kt, g_, :], pt[k][:, :], False, last, (qkv_b, pt_b[k]), (po_b,))
        if last:
            e = (n * 2 + g_) % 2
            sch.I("act", "activation", (po_b,), (den_b[e],), out=rden[e][0:64, :], in_=po[64:128, 0:512], func=AF.Ln)
            sch.I("act", "activation", (den_b[e],), (den_b[e],), out=rden[e][0:64, :], in_=rden[e][0:64, :], func=AF.Exp,
                  scale=-1.0)
            sch.I("dve", "tensor_tensor", (po_b, den_b[e]), (ot_b[e],), out=ot[e][:, :], in0=po[0:64, 0:512], in1=rden[e][:, :],
                  op=ALU.mult)
            sch.dma(Ov[:, g_ * 4:(g_ + 1) * 4, n * 128:(n + 1) * 128], ot[e][:, :].rearrange("d (h q) -> d h q", q=128),
                    reads=(ot_b[e],))

    LOOK = 2
    for idx in range(len(steps) + LOOK):
        if idx < len(steps):
            qk(idx)
        if idx - LOOK >= 0:
            pv(idx - LOOK)
    sch.flush()
    cx.close()


def phase_attn_causal(nc, sch, S, A, H, KD, QT, KT, KR, V, scale, o_row0):
    QC = 512 if S >= 512 else S
    NCH = S // QC
    NB = S // 128
    KPC = QC // 128
    cx = Ctx(nc, sch)
    O = A["O"]
    q_t = [cx.sb([KD, S], BF16, "q") for _ in range(2)]
    k_t = [cx.sb([KD, S], BF16, "k") for _ in range(2)]
    v_t = [cx.sb([128, NB, 2, 128], BF16, "v") for _ in range(2)]
    q_b, k_b, v_b = sch.bufs_n("q", 2), sch.bufs_n("k", 2), sch.bufs_n("v", 2)
    tri = cx.sb([128, 128], BF16, "tri")
    c_b = sch.buf("const")
    PAIR = False
    ps_s = PsRing(cx, sch, 4, "ps_s")
    ps_o = PsRing(cx, sch, 3, "ps_o")
    NPT = 5
    pt = [cx.sb([128, 512], BF16, "pt") for _ in range(NPT)]
    pt_b = sch.bufs_n("pt", NPT)
    rden = [cx.sb([64, 512], F32, "rden") for _ in range(2)]
    ot = [cx.sb([64, 512], BF16, "ot") for _ in range(2)]
    rden_b, ot_b = sch.bufs_n("rden", 2), sch.bufs_n("ot", 2)
    for j_ in range(2):
        sch.I("pool", "memset", (), (v_b[j_],), v_t[j_][:, :, :, 64:128], 1.0)
    sch.dma(tri[:, :], A["tri"], writes=(c_b,))
    Vv = V.rearrange("(n p) f -> p n f", p=128)

    def load_head(h):
        i = h % 2
        sch.dma(q_t[i][:, :], QT[h, :, :], writes=(q_b[i],))
        if KR is None:
            sch.dma(k_t[i][:, :], KT[h, :, :], writes=(k_b[i],))
        else:
            sch.dma(k_t[i][0:64, :], KT[h, :, :], writes=(k_b[i],))
            sch.dma(k_t[i][64:96, :], KR[:, :], writes=(k_b[i],))
        if h % 2 == 0:
            j = (h // 2) % 2
            for hh_ in range(2):
                sch.dma(v_t[j][:, :, hh_, 0:64], Vv[:, :, (h + hh_) * 64:(h + hh_ + 1) * 64], writes=(v_b[j],))

    steps = []
    for h in range(H):
        for c in range(NCH):
            nk = KPC * (c + 1)
            nfull = KPC * c
            j = 0
            while j < nk:
                if PAIR and j + 1 < nfull:
                    steps.append((h, c, (j, j + 1), nk))
                    j += 2
                else:
                    steps.append((h, c, (j,), nk))
                    j += 1
    load_head(0)
    state = {}

    def qk(idx):
        h, c, js, nk = steps[idx]
        i = h % 2
        p, pb = ps_s.next()
        k = idx % NPT
        if len(js) == 2:
            for u, j in enumerate(js):
                sch.mm(p[:, u * 512:u * 512 + QC], k_t[i][:, j * 128:(j + 1) * 128], q_t[i][:, c * QC:(c + 1) * QC], True, True,
                       (q_b[i], k_b[i]), (pb,))
            sch.I("act", "activation", (pb,), (pt_b[k],), out=pt[k][:, 0:1024], in_=p[:, 0:1024], func=AF.Exp,
                  scale=float(scale))
            return
        j = js[0]
        jj = j - KPC * c
        q0 = 128 * jj if jj >= 0 else 0
        n = QC - q0
        sch.mm(p[:, 0:n], k_t[i][:, j * 128:(j + 1) * 128], q_t[i][:, c * QC + q0:(c + 1) * QC], True, True,
               (q_b[i], k_b[i]), (pb,))
        sch.I("act", "activation", (pb,), (pt_b[k],), out=pt[k][:, 0:n], in_=p[:, 0:n], func=AF.Exp, scale=float(scale))
        if jj >= 0:
            sch.I("pool", "tensor_tensor", (pt_b[k], c_b), (pt_b[k],), out=pt[k][:, 0:128], in0=pt[k][:, 0:128], in1=tri[:, :],
                  op=ALU.mult)

    def pv(idx):
        h, c, js, nk = steps[idx]
        i = h % 2
        k = idx % NPT
        if js[0] == 0:
            state["po"], state["po_b"] = ps_o.next()
        po, po_b = state["po"], state["po_b"]
        vj = (h // 2) % 2
        for u, j in enumerate(js):
            jj = j - KPC * c
            q0 = 128 * jj if jj >= 0 else 0
            n = QC - q0
            src = pt[k][:, u * 512:u * 512 + n] if len(js) == 2 else pt[k][:, 0:n]
            sch.mm(po[:, q0:QC], v_t[vj][:, j, h % 2, :], src, j == 0, j == nk - 1, (v_b[vj], pt_b[k]), (po_b,))
        if js[-1] == nk - 1:
            e = (h * NCH + c) % 2
            sch.I("dve", "reciprocal", (po_b,), (rden_b[e],), out=rden[e][:, 0:QC], in_=po[64:128, 0:QC])
            sch.I("dve", "tensor_tensor", (po_b, rden_b[e]), (ot_b[e],), out=ot[e][:, 0:QC], in0=po[0:64, 0:QC],
                  in1=rden[e][:, 0:QC], op=ALU.mult)
            sch.dma(O[o_row0 + h * 64:o_row0 + (h + 1) * 64, c * QC:(c + 1) * QC], ot[e][:, 0:QC], reads=(ot_b[e],))

    LOOK = 3
    for idx in range(len(steps) + LOOK):
        if idx < len(steps):
            h, c, js, nk = steps[idx]
            if c == 0 and js[0] == 0 and h + 1 < H:
                load_head(h + 1)
            qk(idx)
        if idx - LOOK >= 0:
            pv(idx - LOOK)
    sch.flush()
    cx.close()


def phase_oproj(nc, sch, S, h_in, h_out, w_out, A):
    N = 512 if S >= 512 else S
    NT = S // N
    cx = Ctx(nc, sch)
    w = cx.sb([128, KC, D], BF16, "wo")
    w_b = sch.buf("wo")
    h_t = [cx.sb([128, KC, N], F32, "h") for _ in range(2)]
    h_b = sch.bufs_n("h", 2)
    o_t = [cx.sb([128, KC, N], BF16, "o") for _ in range(2)]
    o_b = sch.bufs_n("o", 2)
    pr = PsRing(cx, sch, 4)
    hv = h_in.rearrange("(c p) s -> p c s", p=128)
    ho = h_out.rearrange("(c p) s -> p c s", p=128)
    Ov = A["O"].rearrange("(c p) s -> p c s", p=128)
    sch.dma(h_t[0][:, :, :], hv[:, :, 0:N], writes=(h_b[0],))
    sch.dma(o_t[0][:, :, :], Ov[:, :, 0:N], writes=(o_b[0],))
    wl = WLoader(cx, sch, stage_elems=2048)
    w_r = wl.load(w, w_out, KC, D)
    for t in range(NT):
        i = t % 2
        if t + 1 < NT:
            sch.dma(h_t[1 - i][:, :, :], hv[:, :, (t + 1) * N:(t + 2) * N], writes=(h_b[1 - i],))
            sch.dma(o_t[1 - i][:, :, :], Ov[:, :, (t + 1) * N:(t + 2) * N], writes=(o_b[1 - i],))
        for c in range(KC):
            p, pb = pr.next()
            for f in range(KC):
                sch.mm(p[:, 0:N], w[:, f, c * 128:(c + 1) * 128], o_t[i][:, f, :], f == 0, f == KC - 1,
                       w_r.cols(c * 128, (c + 1) * 128) + (o_b[i],), (pb,))
            sch.I("dve", "tensor_tensor", (pb, h_b[i]), (h_b[i],), out=h_t[i][:, c, :], in0=p[:, 0:N], in1=h_t[i][:, c, :],
                  op=ALU.add)
        sch.dma(ho[:, :, t * N:(t + 1) * N], h_t[i][:, :, :], reads=(h_b[i],))
    sch.flush()
    cx.close()


def phase_ple(nc, sch, S, h_in, h_out, pT, wg_d, wp_d, g_d, final_g=None):
    N = 512 if S >= 512 else S
    NT = S // N
    NH = 3
    cx = Ctx(nc, sch)
    wg = cx.sb([128, KC, D], BF16, "wg")
    wp = cx.sb([128, 2, D], BF16, "wp")
    g_sb = cx.sb([128, KC], F32, "g")
    fg_sb = cx.sb([128, KC], F32, "fg")
    g_b = sch.buf("g")
    T = norm_tiles(cx, sch, KC, N)
    h_t = [cx.sb([128, KC, N], F32, "h") for _ in range(NH)]
    h_b = sch.bufs_n("h", NH)
    p_t = [cx.sb([128, 2, N], F32, "p") for _ in range(2)]
    p_b = sch.bufs_n("p", 2)
    pbf = [cx.sb([128, 2, N], BF16, "pbf") for _ in range(2)]
    pbf_b = sch.bufs_n("pbf", 2)
    xn2 = [cx.sb([128, KC, N], BF16, "xn") for _ in range(2)]
    xn2_b = sch.bufs_n("xn", 2)
    sg = [cx.sb([128, N], F32, "sg") for _ in range(2)]
    tt = [cx.sb([128, N], F32, "tt") for _ in range(2)]
    sg_b, tt_b = sch.bufs_n("sg", 2), sch.bufs_n("tt", 2)
    outt = [cx.sb([128, KC, N], F32, "outt") for _ in range(2)] if final_g is not None else None
    outt_b = sch.bufs_n("outt", 2)
    prg = PsRing(cx, sch, 3, "pg")
    prp = PsRing(cx, sch, 3, "pp")
    hv = h_in.rearrange("(c p) s -> p c s", p=128)
    ho = h_out.rearrange("(c p) s -> p c s", p=128)
    pv = pT.rearrange("(c p) s -> p c s", p=128)
    sch.dma(g_sb[:, :], g_d, writes=(g_b,))
    if final_g is not None:
        sch.dma(fg_sb[:, :], final_g, writes=(g_b,))
    sch.dma(h_t[0][:, :, :], hv[:, :, 0:N], writes=(h_b[0],))
    sch.dma(p_t[0][:, :, :], pv[:, :, 0:N], writes=(p_b[0],))

    def norm(t):
        rmsnorm_stats(sch, h_t[t % NH][:, :, :], h_b[t % NH], KC, N, D, T)
        apply_norm(sch, xn2[t % 2], xn2_b[t % 2], h_t[t % NH], h_b[t % NH], g_sb, g_b, T, KC, N)

    def final_norm(t):
        hi, oi_ = t % NH, t % 2
        rmsnorm_stats(sch, h_t[hi][:, :, :], h_b[hi], KC, N, D, T)
        apply_norm(sch, outt[oi_], outt_b[oi_], h_t[hi], h_b[hi], fg_sb, g_b, T, KC, N)
        sch.dma(ho[:, :, t * N:(t + 1) * N], outt[oi_][:, :, :], reads=(outt_b[oi_],))

    norm(0)
    if NT > 1:
        sch.dma(h_t[1][:, :, :], hv[:, :, N:2 * N], writes=(h_b[1],))
    wl = WLoader(cx, sch, stage_elems=2048)
    wg_r = wl.load(wg, wg_d, KC, D)
    wp_r = wl.load(wp, wp_d, 2, D)
    for t in range(NT):
        i = t % NH
        ip = t % 2
        if t + 1 < NT:
            sch.dma(p_t[1 - ip][:, :, :], pv[:, :, (t + 1) * N:(t + 2) * N], writes=(p_b[1 - ip],))
        def load_ahead():
            if t + 2 < NT:
                sch.dma(h_t[(t + 2) % NH][:, :, :], hv[:, :, (t + 2) * N:(t + 3) * N], writes=(h_b[(t + 2) % NH],))

        if final_g is None or t == 0:
            load_ahead()
        xn, xn_b = xn2[t % 2], xn2_b[t % 2]
        sch.I("act", "copy", (p_b[ip],), (pbf_b[ip],), out=pbf[ip][:, :, :], in_=p_t[ip][:, :, :])
        for c in range(KC):
            j = c % 2
            pgt, pgb = prg.next()
            ppt, ppb = prp.next()
            for f in range(KC):
                sch.mm(pgt[:, 0:N], wg[:, f, c * 128:(c + 1) * 128], xn[:, f, :], f == 0, f == KC - 1,
                       wg_r.cols(c * 128, (c + 1) * 128) + (xn_b,), (pgb,))
            for f in range(2):
                sch.mm(ppt[:, 0:N], wp[:, f, c * 128:(c + 1) * 128], pbf[ip][:, f, :], f == 0, f == 1,
                       wp_r.cols(c * 128, (c + 1) * 128) + (pbf_b[ip],), (ppb,))
            sch.I("act", "activation", (pgb,), (sg_b[j],), out=sg[j][:, :], in_=pgt[:, 0:N], func=AF.Sigmoid)
            sch.I("dve", "tensor_tensor", (ppb, sg_b[j]), (tt_b[j],), out=tt[j][:, :], in0=ppt[:, 0:N], in1=sg[j][:, :],
                  op=ALU.mult)
            sch.I("dve", "tensor_tensor", (tt_b[j], h_b[i]), (h_b[i],), out=h_t[i][:, c, :], in0=h_t[i][:, c, :],
                  in1=tt[j][:, :], op=ALU.add)
            if c == 1 and final_g is not None and t > 0:
                final_norm(t - 1)
                load_ahead()
            if c == 4 and t + 1 < NT:
                norm(t + 1)
        if final_g is None:
            sch.dma(ho[:, :, t * N:(t + 1) * N], h_t[i][:, :, :], reads=(h_b[i],))
    if final_g is not None:
        final_norm(NT - 1)
    sch.flush()
    cx.close()


IN_SPECS = {}


def _in_specs(S):
    sp = {"xT": ([D, S], F32), "pT": ([2, PLE, S], F32), "final_g": ([128, KC], F32),
          "rope_cs": ([32, 2, S], F32), "tri": ([128, 128], BF16), "swa_qaug": ([2, 2, S * 4], BF16), "swa_kaug": ([2, 2, S], BF16), "swa_mask": ([128, 2, 512], BF16), "negi": ([128, 128], BF16),
          "ev_w_in": ([128, KC, 1184], F32), "ev_sinks": ([64, 8], F32), "ev_cq_g": ([128, 2], F32),
          "ev_w_uq": ([128, 2, 768], F32), "ev_ckv_g": ([128, 1], F32), "ev_w_ukv": ([128, 1, 1024], F32),
          "ev_w_out": ([128, KC, D], F32), "od_w_in": ([128, KC, 3088], F32), "od_bf": ([16, 1], F32),
          "od_w_out": ([128, KC, D], F32)}
    for l in range(2):
        for ab in ("a", "b"):
            sp["ff%s_wgu%d" % (ab, l)] = ([128, FC, KC, 256], F32)
            sp["ff%s_wd%d" % (ab, l)] = ([128, FC, D], F32)
            sp["ff%s_g%d" % (ab, l)] = ([128, KC], F32)
        sp["mix_g%d" % l] = ([128, KC], F32)
        sp["ple_g%d" % l] = ([128, KC], F32)
        sp["ple_wg%d" % l] = ([128, KC, D], F32)
        sp["ple_wp%d" % l] = ([128, 2, D], F32)
    return sp


SCRATCH = lambda S: {"hbuf": ([D, S], F32), "QS": ([2, 64, S * 4], BF16), "KS": ([128, S], BF16), "VS": ([S, 128], BF16),
                     "QM": ([8, 96, S], BF16), "KM": ([8, 64, S], BF16), "KR": ([32, S], BF16), "VM": ([S, 512], BF16),
                     "O": ([D, S], BF16), "QF": ([16, 70, S], BF16), "KF": ([16, 70, S], BF16), "VF": ([S, D], BF16)}


def build(S, stop_after=None, dbg=False):
    nc = bass.Bass("TRN2", target_bir_lowering=False)
    stack = contextlib.ExitStack()
    sch = Sched(nc, stack)
    A = {}
    for name, (shape, dtype) in _in_specs(S).items():
        A[name] = nc.dram_tensor(name, list(shape), dtype, kind="ExternalInput").ap()
    for name, (shape, dtype) in SCRATCH(S).items():
        A[name] = nc.dram_tensor(name, list(shape), dtype, kind="ExternalOutput" if dbg else "Internal").ap()
    outT = nc.dram_tensor("outT", [D, S], F32, kind="ExternalOutput").ap()
    h = A["hbuf"]
    ph = []
    ph.append(("ffa0", lambda: phase_ffn(nc, sch, S, A["xT"], h, A["ffa_wgu0"], A["ffa_wd0"], A["ffa_g0"])))
    ph.append(("proj0", lambda: phase_proj_even(nc, sch, S, h, A)))
    ph.append(("swa", lambda: phase_attn_swa(nc, sch, S, A)))
    ph.append(("mla", lambda: phase_attn_causal(nc, sch, S, A, 8, 96, A["QM"], A["KM"], A["KR"], A["VM"], 96.0 ** -0.5, 512)))
    ph.append(("oproj0", lambda: phase_oproj(nc, sch, S, h, h, A["ev_w_out"], A)))
    ph.append(("ffb0", lambda: phase_ffn(nc, sch, S, h, h, A["ffb_wgu0"], A["ffb_wd0"], A["ffb_g0"])))
    ph.append(("ple0", lambda: phase_ple(nc, sch, S, h, h, A["pT"][0], A["ple_wg0"], A["ple_wp0"], A["ple_g0"])))
    ph.append(("ffa1", lambda: phase_ffn(nc, sch, S, h, h, A["ffa_wgu1"], A["ffa_wd1"], A["ffa_g1"])))
    ph.append(("proj1", lambda: phase_proj_odd(nc, sch, S, h, A)))
    ph.append(("fox", lambda: phase_attn_causal(nc, sch, S, A, 16, 70, A["QF"], A["KF"], None, A["VF"], 1.0, 0)))
    ph.append(("oproj1", lambda: phase_oproj(nc, sch, S, h, h, A["od_w_out"], A)))
    ph.append(("ffb1", lambda: phase_ffn(nc, sch, S, h, h, A["ffb_wgu1"], A["ffb_wd1"], A["ffb_g1"])))
    ph.append(("ple1", lambda: phase_ple(nc, sch, S, h, outT, A["pT"][1], A["ple_wg1"], A["ple_wp1"], A["ple_g1"],
                                         final_g=A["final_g"])))
    for name, fn in ph:
        fn()
        if stop_after == name:
            break
    stack.close()
    return nc


def tile_w(w):
    K, M = w.shape
    return np.ascontiguousarray(w.reshape(K // 128, 128, M).transpose(1, 0, 2))


def tile_wgu(w):
    g_ = w[:, :DFF].reshape(KC, 128, FC, 128)
    u_ = w[:, DFF:].reshape(KC, 128, FC, 128)
    gu = np.concatenate([g_, u_], axis=3)
    return np.ascontiguousarray(gu.transpose(1, 2, 0, 3))


def tile_g(g):
    return np.ascontiguousarray(g.reshape(-1, 128).T)


def const_tables(S):
    inv = (10000.0 ** (-np.arange(0, 32, 2, dtype=np.float32) / np.float32(32))).astype(np.float32)
    ang = (np.arange(S, dtype=np.float32)[:, None] * inv[None, :]).astype(np.float32)
    cos, sin = np.cos(ang).astype(np.float32), np.sin(ang).astype(np.float32)
    cos2 = np.concatenate([cos, cos], 1).T
    sin2 = np.concatenate([sin, sin], 1).T
    rope_cs = np.ascontiguousarray(np.stack([cos2, sin2], 1)).astype(np.float32)
    ki = np.arange(128)[:, None]
    qi = np.arange(128)[None, :]
    tri = (ki <= qi).astype(np.float32).astype(ml_dtypes.bfloat16)
    slopes = (2.0 ** (-8.0 * np.arange(1, 9, dtype=np.float32) / 8)).astype(np.float32)
    NB = S // 128
    qaug = np.zeros((2, 2, NB, 4, 128), np.float32)
    for g_ in range(2):
        for hh in range(4):
            sl = slopes[g_ * 4 + hh]
            qaug[g_, 0, :, hh, :] = -sl * np.arange(128, dtype=np.float32)[None, :]
            qaug[g_, 1, :, hh, :] = sl
    qaug = qaug.reshape(2, 2, S * 4).astype(ml_dtypes.bfloat16)
    kaug = np.zeros((2, 2, S), np.float32)
    kin = (np.arange(S) % 128).astype(np.float32)
    kaug[:, 0, :] = 1.0
    kaug[1, 1, :] = kin
    kaug[0, 1, :] = kin - 128.0
    kaug = kaug.astype(ml_dtypes.bfloat16)
    mask = np.zeros((128, 2, 512), np.float32)
    for hh in range(4):
        mask[:, 1, hh * 128:(hh + 1) * 128] = (ki > qi)
        mask[:, 0, hh * 128:(hh + 1) * 128] = (ki <= qi)
    mask = mask.astype(ml_dtypes.bfloat16)
    negi = (-256.0 * np.eye(128, dtype=np.float32)).astype(ml_dtypes.bfloat16)
    return rope_cs, tri, qaug, kaug, mask, negi


def host_inputs(inputs, S):
    f = lambda a: np.ascontiguousarray(np.asarray(a, dtype=np.float32))
    rope_cs, tri, qaug, kaug, mask, negi = const_tables(S)
    sh = {"rope_cs": rope_cs, "tri": tri, "swa_qaug": qaug, "swa_kaug": kaug, "swa_mask": mask, "negi": negi,
          "final_g": tile_g(f(inputs["final_norm"]))}
    ffw = {"a": (inputs["ffa_w_gate_up"], inputs["ffa_w_down"], inputs["ffa_norm"]),
           "b": (inputs["ffb_w_gate_up"], inputs["ffb_w_down"], inputs["ffb_norm"])}
    for l in range(2):
        for ab in ("a", "b"):
            sh["ff%s_wgu%d" % (ab, l)] = tile_wgu(f(ffw[ab][0][l]))
            sh["ff%s_wd%d" % (ab, l)] = tile_w(f(ffw[ab][1][l]))
            sh["ff%s_g%d" % (ab, l)] = tile_g(f(ffw[ab][2][l]))
        sh["mix_g%d" % l] = tile_g(f(inputs["mix_norm"][l]))
        sh["ple_g%d" % l] = tile_g(f(inputs["ple_norm"][l]))
        sh["ple_wg%d" % l] = tile_w(f(inputs["ple_w_gate"][l]))
        sh["ple_wp%d" % l] = tile_w(f(inputs["ple_w_proj"][l]))
    sh["ev_w_in"] = tile_w(f(inputs["ev_w_in"][0]))
    sh["ev_sinks"] = np.ascontiguousarray(np.tile(f(inputs["ev_sinks"][0])[None, :], (64, 1)))
    sh["ev_cq_g"] = tile_g(f(inputs["ev_cq_norm"][0]))
    sh["ev_w_uq"] = tile_w(f(inputs["ev_w_uq"][0]))
    sh["ev_ckv_g"] = tile_g(f(inputs["ev_ckv_norm"][0]))
    sh["ev_w_ukv"] = tile_w(f(inputs["ev_w_ukv"][0]))
    sh["ev_w_out"] = tile_w(f(inputs["ev_w_out"][0]))
    sh["od_w_in"] = tile_w(f(inputs["od_w_in"][0]))
    sh["od_bf"] = f(inputs["od_b_f"][0]).reshape(16, 1)
    sh["od_w_out"] = tile_w(f(inputs["od_w_out"][0]))
    x = f(inputs["x"])
    p = f(inputs["p"])
    B = x.shape[0]
    per = []
    for b in range(B):
        m = dict(sh)
        m["xT"] = np.ascontiguousarray(x[b].T)
        m["pT"] = np.ascontiguousarray(p[:, b].transpose(0, 2, 1))
        per.append(m)
    return per


_NC_CACHE = {}


def kernel(**inputs):
    x = np.asarray(inputs["x"])
    B, S, _ = x.shape
    if S not in _NC_CACHE:
        _NC_CACHE[S] = build(S)
    nc = _NC_CACHE[S]
    in_maps = host_inputs(inputs, S)
    res = run_bass_kernel_spmd(nc, in_maps, core_ids=list(range(B)))
    out = np.stack([np.ascontiguousarray(r["outT"].T) for r in res.results], 0)
    return out.astype(np.float32)
```

```python
import contextlib
import numpy as np
import ml_dtypes
import concourse.bass as bass
import concourse.mybir as mybir
from concourse.bass_utils import run_bass_kernel_spmd

F32 = mybir.dt.float32
BF16 = mybir.dt.bfloat16
AF = mybir.ActivationFunctionType
ALU = mybir.AluOpType

D = 1024
DFF = 2816
FC = DFF // 128
KC = D // 128
PLE = 256
RMS_EPS = 1e-6
N_CORES = 8

SAME_ENGINE_SYNC = True
N_DMA_SEMS = 24


class Buf:
    __slots__ = ("name", "w", "r", "rd")

    def __init__(self, name):
        self.name = name
        self.w = None
        self.r = {}
        self.rd = []


class Op:
    __slots__ = ("eng", "fn", "deps", "is_dma", "event", "clock", "needs_inc", "idx")


class Sched:
    ENGS = ("pe", "act", "dve", "pool", "sp")

    def __init__(self, nc, stack):
        self.nc = nc
        self.sem = {e: stack.enter_context(nc.semaphore("s_" + e)) for e in ("pe", "act", "dve", "pool")}
        self.dma_sems = [stack.enter_context(nc.semaphore("s_dma%d" % i)) for i in range(N_DMA_SEMS)]
        self.count = {e: 0 for e in ("pe", "act", "dve", "pool")}
        self.dma_count = [0] * N_DMA_SEMS
        self.dma_last = [None] * N_DMA_SEMS
        self.dma_i = 0
        self.ops = []
        self.bufs = []
        self.out_events = []

    def buf(self, name):
        b = Buf(name)
        self.bufs.append(b)
        return b

    def bufs_n(self, name, n):
        return [self.buf("%s%d" % (name, i)) for i in range(n)]

    def op(self, eng, fn, reads=(), writes=(), dma=False):
        o = Op()
        o.eng = eng
        o.fn = fn
        o.is_dma = dma
        o.event = None
        o.clock = None
        o.needs_inc = dma
        o.idx = len(self.ops)
        deps = set()
        for b in reads:
            if b.w is not None:
                deps.add(b.w)
        for b in writes:
            if b.w is not None:
                deps.add(b.w)
            for r in b.r.values():
                deps.add(r)
            for r in b.rd:
                deps.add(r)
        if dma:
            slot = self.dma_i % N_DMA_SEMS
            self.dma_i += 1
            if self.dma_last[slot] is not None:
                deps.add(self.dma_last[slot])
            self.dma_last[slot] = o.idx
            o.event = slot
        deps.discard(o.idx)
        fin = set()
        for d in deps:
            od = self.ops[d]
            if od.eng == eng and not od.is_dma:
                if eng == "pe" or eng == "sp" or not SAME_ENGINE_SYNC:
                    continue
            fin.add(d)
        o.deps = fin
        self.ops.append(o)
        for b in reads:
            if dma:
                b.rd.append(o.idx)
            else:
                b.r[eng] = o.idx
        for b in writes:
            b.w = o.idx
            b.r = {}
            b.rd = []
        return o

    def I(self, eng, meth, reads, writes, *args, **kw):
        return self.op(eng, lambda e: getattr(e, meth)(*args, **kw), reads, writes)

    def mm(self, out, lhsT, rhs, start, stop, reads, writes):
        return self.op("pe", lambda e: e.matmul(out, lhsT=lhsT, rhs=rhs, start=start, stop=stop), reads, writes)

    def dma(self, out, in_, reads=(), writes=(), eng="sp"):
        return self.op(eng, lambda e: e.dma_start(out=out, in_=in_), reads, writes, dma=True)

    def flush(self, final=False):
        nc = self.nc
        ops = self.ops
        for o in ops:
            for d in o.deps:
                ops[d].needs_inc = True
        for o in ops:
            if o.is_dma:
                slot = o.event
                self.dma_count[slot] += 16
                o.event = (self.dma_sems[slot], self.dma_count[slot], ("d", slot))
            elif o.needs_inc:
                self.count[o.eng] += 1
                o.event = (self.sem[o.eng], self.count[o.eng], ("e", o.eng))
        known = {e: {} for e in self.ENGS}
        plans = {e: [] for e in self.ENGS}
        for o in ops:
            k = known[o.eng]
            waits = []
            for d in sorted(o.deps):
                od = ops[d]
                sem, val, key = od.event
                if k.get(key, 0) < val:
                    waits.append((sem, val))
                    k[key] = val
                    for kk, vv in od.clock.items():
                        if k.get(kk, 0) < vv:
                            k[kk] = vv
            if o.event is not None:
                o.clock = dict(k)
            plans[o.eng].append((o, waits))
        final_waits = []
        for slot in range(N_DMA_SEMS):
            if self.dma_count[slot] > 0:
                final_waits.append((self.dma_sems[slot], self.dma_count[slot]))

        def emit(engname):
            def body(e):
                for o, waits in plans[engname]:
                    for sem, val in waits:
                        e.wait_ge(sem, val)
                    ins = o.fn(e)
                    if o.event is not None:
                        ins.then_inc(o.event[0], 16 if o.is_dma else 1)
                if engname == "sp":
                    for sem, val in final_waits:
                        e.wait_ge(sem, val)
            return body

        with nc.Block() as block:
            block.tensor(emit("pe"))
            block.scalar(emit("act"))
            block.vector(emit("dve"))
            block.gpsimd(emit("pool"))
            block.sync(emit("sp"))
        self.nops = getattr(self, "nops", 0) + len(ops)
        self.ops = []
        self.dma_last = [None] * N_DMA_SEMS
        for b in self.bufs:
            b.w = None
            b.r = {}
            b.rd = []
        self.bufs = []


class Ctx:
    N = 0

    def __init__(self, nc, sch):
        self.nc = nc
        self.sch = sch
        self.stack = contextlib.ExitStack()
        self.n = 0

    def sb(self, shape, dtype, name=None):
        Ctx.N += 1
        t = self.stack.enter_context(self.nc.sbuf_tensor("%s_%d" % (name or "t", Ctx.N), list(shape), dtype))
        return t

    def ps(self, shape=(128, 512), dtype=F32, name=None):
        Ctx.N += 1
        t = self.stack.enter_context(self.nc.psum_tensor("%s_%d" % (name or "p", Ctx.N), list(shape), dtype))
        return t

    def close(self):
        self.stack.close()


CAST_ENGS = ("pool", "dve", "act")


def cast_op(sch, eng, out, in_, reads, writes, scale=None):
    if eng == "act":
        if scale is None:
            return sch.op("act", lambda e: e.copy(out=out, in_=in_), reads, writes)
        return sch.op("act", lambda e: e.mul(out=out, in_=in_, mul=scale), reads, writes)
    if scale is None:
        return sch.op(eng, lambda e: e.tensor_copy(out=out, in_=in_), reads, writes)
    return sch.op(eng, lambda e: e.tensor_scalar(out=out, in0=in_, scalar1=float(scale), scalar2=None, op0=ALU.mult),
                  reads, writes)


class WRef:
    LOOK = 3

    def __init__(self):
        self.pieces = []
        self.next = 0

    def pump(self, upto):
        upto = min(upto, len(self.pieces) - 1)
        while self.next <= upto:
            p = self.pieces[self.next]
            p[3]()
            p[3] = None
            self.next += 1

    def cols(self, a, b):
        idx = [i for i, (x, y, _, _) in enumerate(self.pieces) if x < b and y > a]
        self.pump(idx[-1] + self.LOOK)
        return tuple(self.pieces[i][2] for i in idx)

    @property
    def all(self):
        self.pump(len(self.pieces) - 1)
        return tuple(p[2] for p in self.pieces)


class WLoader:
    def __init__(self, cx, sch, stage_elems=2048, nstage=4):
        self.sch = sch
        self.stage = [cx.sb([128, stage_elems], F32, "stage") for _ in range(nstage)]
        self.sbuf = sch.bufs_n("stage", nstage)
        self.stage_elems = stage_elems
        self.i = 0
        self.ce = 0

    def load(self, dst, src, kc, M, block=None, prefetch=2):
        sch = self.sch
        ref = WRef()
        w = block or max(64, (self.stage_elems // kc) // 64 * 64)
        m0 = 0
        while m0 < M:
            m1 = min(M, m0 + w)
            buf = sch.buf("wpiece")

            def emit(m0=m0, m1=m1, buf=buf):
                si = self.i % len(self.stage)
                self.i += 1
                eng = ("act", "dve")[self.ce % 2]
                self.ce += 1
                st_ap = self.stage[si][:, 0:kc * (m1 - m0)].rearrange("p (c m) -> p c m", c=kc)
                sch.dma(st_ap, src[:, :, m0:m1], writes=(self.sbuf[si],))
                cast_op(sch, eng, dst[:, :, m0:m1], st_ap, (self.sbuf[si],), (buf,))

            ref.pieces.append([m0, m1, buf, emit])
            m0 = m1
        ref.pump(prefetch - 1)
        return ref


def rmsnorm_stats(sch, h_ap, h_buf, kc, N, dim, T):
    sq, ss, rstd, ones, sd = T["sq"], T["ss"], T["rstd"], T["ones"], T["sd"]
    sch.op("act", lambda e: e.activation(out=sq[:, 0:kc, 0:N], in_=h_ap, func=AF.Square),
           reads=(h_buf,), writes=(T["sq_b"],))
    for c in range(kc):
        sch.op("pe", lambda e, c=c: e.matmul(ss[:, 0:N], lhsT=ones[:, :], rhs=sq[:, c, 0:N],
                                             start=(c == 0), stop=(c == kc - 1)),
               reads=(T["sq_b"], T["ones_b"]), writes=(T["ss_b"],))
    sch.I("act", "activation", (T["ss_b"],), (T["sd_b"],), out=sd[:, 0:N], in_=ss[:, 0:N], func=AF.Ln,
          bias=T["eps"][:, 0:1], scale=1.0 / dim)
    sch.I("act", "activation", (T["sd_b"],), (T["rstd_b"],), out=rstd[:, 0:N], in_=sd[:, 0:N], func=AF.Exp, scale=-0.5)


def norm_tiles(cx, sch, kc, N):
    T = {}
    T["sq"] = cx.sb([128, kc, N], BF16, "sq")
    T["sq_b"] = sch.buf("sq")
    T["ss"] = cx.ps([128, 512], F32, "ss")
    T["ss_b"] = sch.buf("ss")
    T["rstd"] = cx.sb([128, N], F32, "rstd")
    T["rstd_b"] = sch.buf("rstd")
    T["sd"] = cx.sb([128, N], F32, "sd")
    T["sd_b"] = sch.buf("sd")
    T["ones"] = cx.sb([128, 128], BF16, "ones")
    T["ones_b"] = sch.buf("ones")
    T["eps"] = cx.sb([128, 1], F32, "eps")
    sch.op("pool", lambda e: e.memset(T["ones"][:, :], 1.0), writes=(T["ones_b"],))
    sch.op("pool", lambda e: e.memset(T["eps"][:, :], RMS_EPS), writes=(T["sd_b"],))
    return T


def apply_norm(sch, xn, xn_buf, h_t, h_buf, g_sb, g_buf, T, kc, N, eng="dve"):
    rstd = T["rstd"]
    for c in range(kc):
        sch.op(eng, lambda e, c=c: e.scalar_tensor_tensor(out=xn[:, c, 0:N], in0=h_t[:, c, 0:N],
                                                          scalar=g_sb[:, c:c + 1], in1=rstd[:, 0:N],
                                                          op0=ALU.mult, op1=ALU.mult),
               reads=(h_buf, g_buf, T["rstd_b"]), writes=(xn_buf,))


def phase_ffn(nc, sch, S, h_in, h_out, wgu, wd, g):
    N = 256 if S >= 256 else S
    NT = S // N
    cx = Ctx(nc, sch)
    wgu_bf = cx.sb([128, FC, KC, 256], BF16, "wgu")
    wd_bf = cx.sb([128, FC, D], BF16, "wd")
    wgu_b = sch.bufs_n("wgu", FC)
    wd_b = sch.bufs_n("wd", FC // 2)
    NST = 3
    stage = [cx.sb([128, 2048], F32, "stage") for _ in range(NST)]
    stage_b = sch.bufs_n("stage", NST)
    g_sb = cx.sb([128, KC], F32, "g")
    g_b = sch.buf("g")
    T = norm_tiles(cx, sch, KC, N)
    NH = 3
    h_t = [cx.sb([128, KC, N], F32, "h") for _ in range(NH)]
    h_b = sch.bufs_n("h", NH)
    xn = [cx.sb([128, KC, N], BF16, "xn") for _ in range(2)]
    xn_b = sch.bufs_n("xn", 2)
    act = cx.sb([128, FC, N], BF16, "act")
    act_b = sch.bufs_n("act", FC)
    sil = [cx.sb([128, N], F32, "sil") for _ in range(2)]
    sil_b = sch.bufs_n("sil", 2)
    pg = [cx.ps() for _ in range(2)]
    pu = [cx.ps() for _ in range(2)]
    po = [cx.ps() for _ in range(2)]
    pg_b, pu_b, po_b = sch.bufs_n("pg", 2), sch.bufs_n("pu", 2), sch.bufs_n("po", 2)
    hv = h_in.rearrange("(c p) s -> p c s", p=128)
    ho = h_out.rearrange("(c p) s -> p c s", p=128)

    sch.dma(g_sb[:, :], g, writes=(g_b,))
    sch.dma(h_t[0][:, :, :], hv[:, :, 0:N], writes=(h_b[0],))
    engs = ("act", "dve", "act")
    pieces = []
    for f in range(FC):
        pieces.append(("gu", f))
        if f % 2 == 1:
            pieces.append(("d", f // 2))
    pstate = {"i": 0}

    def emit_piece():
        if pstate["i"] >= len(pieces):
            return
        kind, ix = pieces[pstate["i"]]
        k = pstate["i"] % NST
        eng = engs[pstate["i"] % 3]
        pstate["i"] += 1
        if kind == "gu":
            sch.dma(stage[k][:, :], wgu[:, ix, :, :].rearrange("p c j -> p (c j)"), writes=(stage_b[k],))
            cast_op(sch, eng, wgu_bf[:, ix, :, :].rearrange("p c j -> p (c j)"), stage[k][:, :], (stage_b[k],), (wgu_b[ix],))
        else:
            sch.dma(stage[k][:, :], wd[:, 2 * ix:2 * ix + 2, :].rearrange("p f d -> p (f d)"), writes=(stage_b[k],))
            cast_op(sch, eng, wd_bf[:, 2 * ix:2 * ix + 2, :].rearrange("p f d -> p (f d)"), stage[k][:, :], (stage_b[k],),
                    (wd_b[ix],))

    def norm(t):
        rmsnorm_stats(sch, h_t[t % NH][:, :, :], h_b[t % NH], KC, N, D, T)
        apply_norm(sch, xn[t % 2], xn_b[t % 2], h_t[t % NH], h_b[t % NH], g_sb, g_b, T, KC, N)

    norm(0)
    if NT > 1:
        sch.dma(h_t[1][:, :, :], hv[:, :, N:2 * N], writes=(h_b[1],))
    for _ in range(NST):
        emit_piece()
    for t in range(NT):
        i = t % 2
        ih = t % NH
        if t + 2 < NT:
            sch.dma(h_t[(t + 2) % NH][:, :, :], hv[:, :, (t + 2) * N:(t + 3) * N], writes=(h_b[(t + 2) % NH],))
        for f in range(FC):
            j = f % 2
            if t == 0:
                emit_piece()
                emit_piece()
            for c in range(KC):
                sch.mm(pg[j][:, 0:N], wgu_bf[:, f, c, 0:128], xn[i][:, c, :], c == 0, c == KC - 1,
                       (wgu_b[f], xn_b[i]), (pg_b[j],))
            for c in range(KC):
                sch.mm(pu[j][:, 0:N], wgu_bf[:, f, c, 128:256], xn[i][:, c, :], c == 0, c == KC - 1,
                       (wgu_b[f], xn_b[i]), (pu_b[j],))
            sch.I("act", "activation", (pg_b[j],), (sil_b[j],), out=sil[j][:, :], in_=pg[j][:, 0:N], func=AF.Silu)
            sch.I("dve", "tensor_tensor", (pu_b[j], sil_b[j]), (act_b[f],), out=act[:, f, :], in0=pu[j][:, 0:N],
                  in1=sil[j][:, :], op=ALU.mult)
            if t + 1 < NT:
                i2 = (t + 1) % 2
                ih2 = (t + 1) % NH
                if f == 5:
                    sch.I("act", "activation", (h_b[ih2],), (T["sq_b"],), out=T["sq"][:, 0:KC, 0:N], in_=h_t[ih2][:, :, :],
                          func=AF.Square)
                elif f == 9:
                    for c in range(KC):
                        sch.mm(T["ss"][:, 0:N], T["ones"][:, :], T["sq"][:, c, 0:N], c == 0, c == KC - 1,
                               (T["sq_b"], T["ones_b"]), (T["ss_b"],))
                    sch.I("act", "activation", (T["ss_b"],), (T["sd_b"],), out=T["sd"][:, 0:N], in_=T["ss"][:, 0:N],
                          func=AF.Ln, bias=T["eps"][:, 0:1], scale=1.0 / D)
                    sch.I("act", "activation", (T["sd_b"],), (T["rstd_b"],), out=T["rstd"][:, 0:N], in_=T["sd"][:, 0:N],
                          func=AF.Exp, scale=-0.5)
                elif 11 <= f < 11 + KC:
                    c = f - 11
                    sch.I("dve", "scalar_tensor_tensor", (h_b[ih2], g_b, T["rstd_b"]), (xn_b[i2],), out=xn[i2][:, c, 0:N],
                          in0=h_t[ih2][:, c, 0:N], scalar=g_sb[:, c:c + 1], in1=T["rstd"][:, 0:N], op0=ALU.mult, op1=ALU.mult)
        for c in range(KC):
            j = c % 2
            for f in range(FC):
                sch.mm(po[j][:, 0:N], wd_bf[:, f, c * 128:(c + 1) * 128], act[:, f, :], f == 0, f == FC - 1,
                       (wd_b[f // 2], act_b[f]), (po_b[j],))
            sch.I("dve", "scalar_tensor_tensor", (po_b[j], h_b[ih]), (h_b[ih],), out=h_t[ih][:, c, :], in0=po[j][:, 0:N],
                  scalar=0.5, in1=h_t[ih][:, c, :], op0=ALU.mult, op1=ALU.add)
        sch.dma(ho[:, :, t * N:(t + 1) * N], h_t[ih][:, :, :], reads=(h_b[ih],))
    sch.flush()
    cx.close()


class PsRing:
    def __init__(self, cx, sch, n, name="pp", shape=(128, 512)):
        self.t = [cx.ps(shape) for _ in range(n)]
        self.b = sch.bufs_n(name, n)
        self.i = 0

    def next(self):
        k = self.i % len(self.t)
        self.i += 1
        return self.t[k], self.b[k]


class EvacRR:
    def __init__(self, sch, pattern=("act", "dve")):
        self.sch = sch
        self.i = 0
        self.pattern = pattern

    def copy(self, out, in_, reads, writes, scale=None, eng=None):
        if eng is None:
            eng = self.pattern[self.i % len(self.pattern)]
            self.i += 1
        return cast_op(self.sch, eng, out, in_, reads, writes, scale)


def phase_proj_even(nc, sch, S, h_in, A):
    N = 512 if S >= 512 else S
    NT = S // N
    NS = N // 128
    cx = Ctx(nc, sch)
    W_IN = 1184
    win = cx.sb([128, KC, W_IN], BF16, "win")
    win_b = sch.buf("win")
    wkr = cx.sb([128, KC, 96], BF16, "wkr")
    wkrr = cx.sb([128, KC, 96], BF16, "wkrr")
    wkr_b, wkrr_b = sch.buf("wkr"), sch.buf("wkrr")
    wuq = cx.sb([128, 2, 768], BF16, "wuq")
    wuqr = cx.sb([128, 2, 8, 96], BF16, "wuqr")
    wuq_b, wuqr_b = sch.buf("wuq"), sch.buf("wuqr")
    wukv = cx.sb([128, 1, 1024], BF16, "wukv")
    wukv_b = sch.buf("wukv")
    wkn = cx.sb([128, 8, 64], BF16, "wkn")
    wvm = cx.sb([128, 8, 64], BF16, "wvm")
    wkn_b, wvm_b = sch.buf("wkn"), sch.buf("wvm")
    g_sb = cx.sb([128, KC], F32, "g")
    gq_sb = cx.sb([128, 2], F32, "gq")
    gkv_sb = cx.sb([128, 1], F32, "gkv")
    g_b = sch.buf("gs")
    T = norm_tiles(cx, sch, KC, N)
    h_t = [cx.sb([128, KC, N], F32, "h") for _ in range(2)]
    h_b = sch.bufs_n("h", 2)
    xn2 = [cx.sb([128, KC, N], BF16, "xn") for _ in range(2)]
    xn2_b = sch.bufs_n("xn", 2)
    cs = [cx.sb([96, 2, N], F32, "cs") for _ in range(2)]
    cs_b = sch.bufs_n("cs", 2)
    pr = PsRing(cx, sch, 6)
    ev = EvacRR(sch, pattern=("act", "act", "act", "dve"))
    NOUT = 6
    ob = [cx.sb([128, N], BF16, "ob") for _ in range(NOUT)]
    ob_b = sch.bufs_n("ob", NOUT)
    oi = [0]

    def nxt_ob():
        k = oi[0] % NOUT
        oi[0] += 1
        return ob[k], ob_b[k]

    cq = cx.sb([128, 2, N], F32, "cq")
    cq_b = sch.buf("cq")
    cqn = cx.sb([128, 2, N], BF16, "cqn")
    cqn_b = sch.buf("cqn")
    ckv = cx.sb([128, 1, N], F32, "ckv")
    ckv_b = sch.buf("ckv")
    ckvn = cx.sb([128, 1, N], BF16, "ckvn")
    ckvn_b = sch.buf("ckvn")
    t1 = [cx.sb([96, N], F32, "t1") for _ in range(2)]
    t2 = [cx.sb([96, N], F32, "t2") for _ in range(2)]
    t1_b, t2_b = sch.bufs_n("t1", 2), sch.bufs_n("t2", 2)
    hv = h_in.rearrange("(c p) s -> p c s", p=128)

    sch.dma(g_sb[:, :], A["mix_g0"], writes=(g_b,))
    sch.dma(gq_sb[:, :], A["ev_cq_g"], writes=(g_b,))
    sch.dma(gkv_sb[:, :], A["ev_ckv_g"], writes=(g_b,))
    sch.dma(h_t[0][:, :, :], hv[:, :, 0:N], writes=(h_b[0],))
    sch.dma(cs[0][64:96, :, :], A["rope_cs"][:, :, 0:N], writes=(cs_b[0],))
    def norm(t):
        i_ = t % 2
        rmsnorm_stats(sch, h_t[i_][:, :, :], h_b[i_], KC, N, D, T)
        apply_norm(sch, xn2[i_], xn2_b[i_], h_t[i_], h_b[i_], g_sb, g_b, T, KC, N)

    norm(0)
    sch.I("pool", "memset", (), (wkr_b,), wkr[:, :, :], 0.0)
    sch.I("pool", "memset", (), (wkrr_b,), wkrr[:, :, :], 0.0)
    sch.I("pool", "memset", (), (wuqr_b,), wuqr[:, :, :, :], 0.0)
    wl = WLoader(cx, sch, stage_elems=2048)
    win_r = wl.load(win, A["ev_w_in"], KC, W_IN)
    wuq_r = wl.load(wuq, A["ev_w_uq"], 2, 768, prefetch=0)
    wukv_r = wl.load(wukv, A["ev_w_ukv"], 1, 1024, prefetch=0)
    wuq4 = wuq[:, :, :].rearrange("p c (h e) -> p c h e", e=96)
    wukv4 = wukv[:, 0, :].rearrange("p (h e) -> p h e", e=128)

    def derive_uq():
        sch.I("dve", "tensor_scalar", wuq_r.all, (wuqr_b,), out=wuqr[:, :, :, 64:80], in0=wuq4[:, :, :, 80:96], scalar1=-1.0,
              scalar2=None, op0=ALU.mult)
        sch.I("act", "copy", wuq_r.all, (wuqr_b,), out=wuqr[:, :, :, 80:96], in_=wuq4[:, :, :, 64:80])

    def derive_kv():
        sch.I("dve", "tensor_copy", wukv_r.all, (wkn_b,), out=wkn[:, :, :], in_=wukv4[:, :, 0:64])
        sch.I("act", "copy", wukv_r.all, (wvm_b,), out=wvm[:, :, :], in_=wukv4[:, :, 64:128])

    def derive_kr():
        kr_c = win_r.cols(1152, 1184)
        sch.I("act", "copy", kr_c, (wkr_b,), out=wkr[:, :, 64:96], in_=win[:, :, 1152:1184])
        sch.I("dve", "tensor_scalar", kr_c, (wkrr_b,), out=wkrr[:, :, 64:80], in0=win[:, :, 1168:1184], scalar1=-1.0,
              scalar2=None, op0=ALU.mult)
        sch.I("act", "copy", kr_c, (wkrr_b,), out=wkrr[:, :, 80:96], in_=win[:, :, 1152:1168])

    QS, KS, VS, QM, KM, KR, VM = A["QS"], A["KS"], A["VS"], A["QM"], A["KM"], A["KR"], A["VM"]
    QSv = QS.rearrange("g d (n h q) -> g d n h q", h=4, q=128)
    VSv = VS.rearrange("(n p) f -> p n f", p=128)
    VMv = VM.rearrange("(n p) f -> p n f", p=128)

    def small_norm(src, src_b, kc, dim, g_ap, dst, dst_b):
        rmsnorm_stats(sch, src[:, 0:kc, :], src_b, kc, N, dim, T)
        for c in range(kc):
            sch.I("dve", "scalar_tensor_tensor", (src_b, g_b, T["rstd_b"]), (dst_b,), out=dst[:, c, :], in0=src[:, c, :],
                  scalar=g_ap[:, c:c + 1], in1=T["rstd"][:, 0:N], op0=ALU.mult, op1=ALU.mult)

    cur = {}

    def lin(col0, m, w=None, wb=None, kc=KC, rhs=None, rhs_b=None):
        if w is None:
            w, wb = win, win_r.cols(col0, col0 + m)
        rhs = cur["xn"] if rhs is None else rhs
        rhs_b = cur["xn_b"] if rhs_b is None else rhs_b
        p, pb = pr.next()
        for c in range(kc):
            sch.mm(p[0:m, 0:N], w[:, c, col0:col0 + m], rhs[:, c, :], c == 0, c == kc - 1, tuple(wb) + (rhs_b,), (pb,))
        return p, pb

    for t in range(NT):
        i = t % 2
        tok = slice(t * N, (t + 1) * N)
        if t + 1 < NT:
            sch.dma(h_t[1 - i][:, :, :], hv[:, :, (t + 1) * N:(t + 2) * N], writes=(h_b[1 - i],))
            sch.dma(cs[1 - i][64:96, :, :], A["rope_cs"][:, :, (t + 1) * N:(t + 2) * N], writes=(cs_b[1 - i],))
        xn, xn_b = xn2[i], xn2_b[i]
        cur["xn"], cur["xn_b"] = xn, xn_b
        for c2 in range(2):
            p, pb = lin(768 + c2 * 128, 128)
            ev.copy(cq[:, c2, :], p[:, 0:N], (pb,), (cq_b,))
        p, pb = lin(1024, 128)
        ev.copy(ckv[:, 0, :], p[:, 0:N], (pb,), (ckv_b,))

        def swa_q(f):
            p, pb = lin(f * 128, 128)
            o, o_b = nxt_ob()
            ev.copy(o[:, :], p[:, 0:N], (pb,), (o_b,), scale=0.125)
            for hh2 in range(2):
                head = 2 * f + hh2
                g_, hh = head // 4, head % 4
                sch.dma(QSv[g_, :, t * NS:(t + 1) * NS, hh, :], o[hh2 * 64:(hh2 + 1) * 64, :].rearrange("p (n q) -> p n q", q=128),
                        reads=(o_b,))

        swa_q(0)
        swa_q(1)
        small_norm(cq, cq_b, 2, 256, gq_sb, cqn, cqn_b)
        swa_q(2)
        swa_q(3)
        small_norm(ckv, ckv_b, 1, 128, gkv_sb, ckvn, ckvn_b)
        p, pb = lin(512, 128)
        o, o_b = nxt_ob()
        ev.copy(o[:, :], p[:, 0:N], (pb,), (o_b,))
        sch.dma(KS[:, tok], o[:, :], reads=(o_b,))
        p, pb = pr.next()
        for s_ in range(NS):
            for c in range(KC):
                sch.mm(p[:, s_ * 128:(s_ + 1) * 128], xn[:, c, s_ * 128:(s_ + 1) * 128], win[:, c, 640:768], c == 0,
                       c == KC - 1, win_r.cols(640, 768) + (xn_b,), (pb,))
        o, o_b = nxt_ob()
        ev.copy(o[:, 0:NS * 128], p[:, 0:NS * 128], (pb,), (o_b,))
        sch.dma(VSv[:, t * NS:(t + 1) * NS, :], o[:, 0:NS * 128].rearrange("p (n f) -> p n f", f=128), reads=(o_b,))

        def rope_rows(pq, pq_b, prr, prr_b, o, o_b):
            k = oi[0] % 2
            sch.I("dve", "tensor_tensor", (pq_b, cs_b[i]), (t1_b[k],), out=t1[k][64:96, :], in0=pq[64:96, 0:N],
                  in1=cs[i][64:96, 0, :], op=ALU.mult)
            sch.I("dve", "tensor_tensor", (prr_b, cs_b[i]), (t2_b[k],), out=t2[k][64:96, :], in0=prr[64:96, 0:N],
                  in1=cs[i][64:96, 1, :], op=ALU.mult)
            sch.I("dve", "tensor_tensor", (t1_b[k], t2_b[k]), (o_b,), out=o[64:96, :], in0=t1[k][64:96, :],
                  in1=t2[k][64:96, :], op=ALU.add)

        if t == 0:
            derive_uq()
        for hd in range(8):
            pq, pq_b = lin(hd * 96, 96, w=wuq, wb=wuq_r.all, kc=2, rhs=cqn, rhs_b=cqn_b)
            prr, prr_b = pr.next()
            for c in range(2):
                sch.mm(prr[0:96, 0:N], wuqr[:, c, hd, :], cqn[:, c, :], c == 0, c == 1, (wuqr_b, cqn_b), (prr_b,))
            o, o_b = nxt_ob()
            sch.I("act", "copy", (pq_b,), (o_b,), out=o[0:64, :], in_=pq[0:64, 0:N])
            rope_rows(pq, pq_b, prr, prr_b, o, o_b)
            sch.dma(QM[hd, :, tok], o[0:96, :], reads=(o_b,))
        if t + 1 < NT:
            norm(t + 1)
        if t == 0:
            derive_kv()
        for hp in range(4):
            p, pb = pr.next()
            sch.mm(p[:, 0:N], wkn[:, 2 * hp:2 * hp + 2, :], ckvn[:, 0, :], True, True, (wkn_b, ckvn_b), (pb,))
            o, o_b = nxt_ob()
            ev.copy(o[:, :], p[:, 0:N], (pb,), (o_b,))
            sch.dma(KM[2 * hp, :, tok], o[0:64, :], reads=(o_b,))
            sch.dma(KM[2 * hp + 1, :, tok], o[64:128, :], reads=(o_b,))
        for s_ in range(NS):
            p, pb = pr.next()
            sch.mm(p[:, 0:512], ckvn[:, 0, s_ * 128:(s_ + 1) * 128], wvm[:, :, :], True, True, (wvm_b, ckvn_b), (pb,))
            o, o_b = nxt_ob()
            ev.copy(o[:, 0:512], p[:, 0:512], (pb,), (o_b,))
            sch.dma(VMv[:, t * NS + s_, :], o[:, 0:512], reads=(o_b,))
        if t == 0:
            derive_kr()
        pq, pq_b = lin(0, 96, w=wkr, wb=(wkr_b,))
        prr, prr_b = lin(0, 96, w=wkrr, wb=(wkrr_b,))
        o, o_b = nxt_ob()
        rope_rows(pq, pq_b, prr, prr_b, o, o_b)
        sch.dma(KR[:, tok], o[64:96, :], reads=(o_b,))
    sch.flush()
    cx.close()


def phase_proj_odd(nc, sch, S, h_in, A):
    N = 512 if S >= 512 else S
    NT = S // N
    NS = N // 128
    cx = Ctx(nc, sch)
    W_IN = 3088
    win = cx.sb([128, KC, W_IN], BF16, "win")
    win_b = sch.buf("win")
    g_sb = cx.sb([128, KC], F32, "g")
    g_b = sch.buf("gs")
    nbf = cx.sb([16, 1], F32, "nbf")
    nbf_b = sch.buf("nbf")
    T = norm_tiles(cx, sch, KC, N)
    h_t = [cx.sb([128, KC, N], F32, "h") for _ in range(2)]
    h_b = sch.bufs_n("h", 2)
    xn2 = [cx.sb([128, KC, N], BF16, "xn") for _ in range(2)]
    xn2_b = sch.bufs_n("xn", 2)
    pr = PsRing(cx, sch, 6)
    ev = EvacRR(sch)
    NOUT = 6
    ob = [cx.sb([128, N], BF16, "ob") for _ in range(NOUT)]
    ob_b = sch.bufs_n("ob", NOUT)
    oi = [0]

    def nxt_ob():
        k = oi[0] % NOUT
        oi[0] += 1
        return ob[k], ob_b[k]

    ones16 = cx.sb([16, N], F32, "ones16")
    ones3 = cx.sb([16, 3, N], BF16, "ones3")
    on_b = sch.buf("ones16")
    e1 = cx.sb([16, N], F32, "e1")
    lf = cx.sb([16, N], F32, "lf")
    e1_b, lf_b = sch.buf("e1"), sch.buf("lf")
    cc = [cx.sb([16, N], F32, "cc") for _ in range(2)]
    cc_b = sch.bufs_n("cc", 2)
    pc = [cx.sb([16, 3, N], BF16, "pc") for _ in range(2)]
    nc_ = [cx.sb([16, 3, N], BF16, "ncs") for _ in range(2)]
    pc_b, nc_b = sch.bufs_n("pc", 2), sch.bufs_n("ncs", 2)
    r32 = cx.sb([16, N], F32, "r32")
    r1 = cx.sb([16, N], F32, "r1")
    r2 = cx.sb([16, N], F32, "r2")
    r_b = sch.buf("r")
    hv = h_in.rearrange("(c p) s -> p c s", p=128)
    QF, KF, VF = A["QF"], A["KF"], A["VF"]
    VFv = VF.rearrange("(n p) f -> p n f", p=128)

    sch.dma(g_sb[:, :], A["mix_g1"], writes=(g_b,))
    sch.dma(nbf[:, :], A["od_bf"], writes=(nbf_b,))
    sch.I("dve", "tensor_scalar", (nbf_b,), (nbf_b,), out=nbf[:, :], in0=nbf[:, :], scalar1=-1.0, scalar2=None, op0=ALU.mult)
    sch.I("pool", "memset", (), (on_b,), ones16[:, :], 1.0)
    sch.I("pool", "memset", (), (on_b,), ones3[:, :, :], 1.0)
    sch.dma(h_t[0][:, :, :], hv[:, :, 0:N], writes=(h_b[0],))
    def norm(t):
        i_ = t % 2
        rmsnorm_stats(sch, h_t[i_][:, :, :], h_b[i_], KC, N, D, T)
        apply_norm(sch, xn2[i_], xn2_b[i_], h_t[i_], h_b[i_], g_sb, g_b, T, KC, N)

    norm(0)
    wl = WLoader(cx, sch, stage_elems=2048)
    win_r = wl.load(win[:, :, 3072:3088], A["od_w_in"][:, :, 3072:3088], KC, 16)
    win_r2 = wl.load(win[:, :, 0:3072], A["od_w_in"][:, :, 0:3072], KC, 3072)
    cur = {}

    def wcols(a, b_):
        if a >= 3072:
            return win_r.all
        return win_r2.cols(a, b_)

    def lin(col0, m):
        p, pb = pr.next()
        xn, xn_b = cur["xn"], cur["xn_b"]
        for c in range(KC):
            sch.mm(p[0:m, 0:N], win[:, c, col0:col0 + m], xn[:, c, :], c == 0, c == KC - 1, wcols(col0, col0 + m) + (xn_b,),
                   (pb,))
        return p, pb

    for t in range(NT):
        i = t % 2
        tok = slice(t * N, (t + 1) * N)
        if t + 1 < NT:
            sch.dma(h_t[1 - i][:, :, :], hv[:, :, (t + 1) * N:(t + 2) * N], writes=(h_b[1 - i],))
        xn, xn_b = xn2[i], xn2_b[i]
        cur["xn"], cur["xn_b"] = xn, xn_b
        p, pb = lin(3072, 16)
        sch.I("act", "activation", (pb, nbf_b), (e1_b,), out=e1[:, :], in_=p[0:16, 0:N], func=AF.Exp, bias=nbf[:, 0:1], scale=-1.0)
        sch.I("act", "activation", (e1_b,), (lf_b,), out=lf[:, :], in_=e1[:, :], func=AF.Ln, bias=1.0, scale=1.0)
        init = 0.0 if t == 0 else cc[1 - i][:, N - 1:N]
        sch.I("dve", "tensor_tensor_scan", (lf_b, on_b, cc_b[1 - i]), (cc_b[i],), out=cc[i][:, :], data0=ones16[:, :],
              data1=lf[:, :], initial=init, op0=ALU.mult, op1=ALU.subtract)
        P_, Nn = pc[i], nc_[i]
        sch.I("dve", "tensor_copy", (cc_b[i],), (pc_b[i],), out=P_[:, 0, :], in_=cc[i][:, :])
        sch.I("dve", "tensor_copy", (pc_b[i],), (r_b,), out=r32[:, :], in_=P_[:, 0, :])
        sch.I("dve", "tensor_tensor", (cc_b[i], r_b), (r_b,), out=r1[:, :], in0=cc[i][:, :], in1=r32[:, :], op=ALU.subtract)
        sch.I("dve", "tensor_copy", (r_b,), (pc_b[i],), out=P_[:, 1, :], in_=r1[:, :])
        sch.I("dve", "tensor_copy", (pc_b[i],), (r_b,), out=r32[:, :], in_=P_[:, 1, :])
        sch.I("dve", "tensor_tensor", (r_b,), (r_b,), out=r2[:, :], in0=r1[:, :], in1=r32[:, :], op=ALU.subtract)
        sch.I("dve", "tensor_copy", (r_b,), (pc_b[i],), out=P_[:, 2, :], in_=r2[:, :])
        sch.I("act", "mul", (pc_b[i],), (nc_b[i],), out=Nn[:, :, :], in_=P_[:, :, :], mul=-1.0)
        sch.dma(QF[:, 64:67, tok], P_[:, :, :], reads=(pc_b[i],))
        sch.dma(KF[:, 67:70, tok], Nn[:, :, :], reads=(nc_b[i],))
        sch.dma(QF[:, 67:70, tok], ones3[:, :, :], reads=(on_b,))
        sch.dma(KF[:, 64:67, tok], ones3[:, :, :], reads=(on_b,))
        for f in range(8):
            p, pb = lin(f * 128, 128)
            o, o_b = nxt_ob()
            ev.copy(o[:, :], p[:, 0:N], (pb,), (o_b,), scale=0.125)
            sch.dma(QF[2 * f, 0:64, tok], o[0:64, :], reads=(o_b,))
            sch.dma(QF[2 * f + 1, 0:64, tok], o[64:128, :], reads=(o_b,))
        if t + 1 < NT:
            norm(t + 1)
        for f in range(8):
            p, pb = lin(1024 + f * 128, 128)
            o, o_b = nxt_ob()
            ev.copy(o[:, :], p[:, 0:N], (pb,), (o_b,))
            sch.dma(KF[2 * f, 0:64, tok], o[0:64, :], reads=(o_b,))
            sch.dma(KF[2 * f + 1, 0:64, tok], o[64:128, :], reads=(o_b,))
        for s_ in range(NS):
            for half in range(2):
                p, pb = pr.next()
                for c in range(KC):
                    sch.mm(p[:, 0:512], xn[:, c, s_ * 128:(s_ + 1) * 128], win[:, c, 2048 + half * 512:2048 + (half + 1) * 512],
                           c == 0, c == KC - 1, wcols(2048 + half * 512, 2048 + (half + 1) * 512) + (xn_b,), (pb,))
                o, o_b = nxt_ob()
                ev.copy(o[:, 0:512], p[:, 0:512], (pb,), (o_b,))
                sch.dma(VFv[:, t * NS + s_, half * 512:(half + 1) * 512], o[:, 0:512], reads=(o_b,))
    sch.flush()
    cx.close()


def phase_attn_swa(nc, sch, S, A):
    NB = S // 128
    cx = Ctx(nc, sch)
    QS, KS, VS, O = A["QS"], A["KS"], A["VS"], A["O"]
    q_t = [cx.sb([66, S * 4], BF16, "q") for _ in range(2)]
    k_c = [cx.sb([66, S], BF16, "kc") for _ in range(2)]
    k_p = [cx.sb([66, S], BF16, "kp") for _ in range(2)]
    v_t = cx.sb([128, NB, 2, 128], BF16, "v")
    mask = cx.sb([128, 2, 512], BF16, "mask")
    negi = cx.sb([128, 128], BF16, "negi")
    qkv_b = sch.buf("qkv")
    es = cx.sb([1, 8], F32, "es")
    esr = cx.sb([1, 2, 512], BF16, "esr")
    zer = cx.sb([1, 128], F32, "zer")
    selz = cx.sb([1, 128], BF16, "selz")
    c_b = sch.buf("const")
    ps_s = PsRing(cx, sch, 4, "ps_s")
    ps_o = PsRing(cx, sch, 3, "ps_o")
    NSB = 4
    pt = [cx.sb([128, 512], BF16, "pt") for _ in range(NSB)]
    pt_b = sch.bufs_n("pt", NSB)
    rden = [cx.sb([64, 512], F32, "rden") for _ in range(2)]
    ot = [cx.sb([64, 512], BF16, "ot") for _ in range(2)]
    den_b, ot_b = sch.bufs_n("den", 2), sch.bufs_n("ot", 2)

    sch.I("pool", "memset", (), (qkv_b,), v_t[:, :, :, 64:128], 1.0)
    for g_ in range(2):
        sch.dma(q_t[g_][0:64, :], QS[g_, :, :], writes=(qkv_b,))
        sch.dma(q_t[g_][64:66, :], A["swa_qaug"][g_, :, :], writes=(qkv_b,))
        sch.dma(k_c[g_][0:64, :], KS[g_ * 64:(g_ + 1) * 64, :], writes=(qkv_b,))
        sch.dma(k_p[g_][0:64, :], KS[g_ * 64:(g_ + 1) * 64, :], writes=(qkv_b,))
        sch.dma(k_c[g_][64:66, :], A["swa_kaug"][1, :, :], writes=(qkv_b,))
        sch.dma(k_p[g_][64:66, :], A["swa_kaug"][0, :, :], writes=(qkv_b,))
    VSv = VS.rearrange("(n p) f -> p n f", p=128)
    for g_ in range(2):
        sch.dma(v_t[:, :, g_, 0:64], VSv[:, :, g_ * 64:(g_ + 1) * 64], writes=(qkv_b,))
    sch.dma(mask[:, :, :], A["swa_mask"], writes=(c_b,))
    sch.dma(negi[:, :], A["negi"], writes=(c_b,))
    sch.dma(es[:, :], A["ev_sinks"][0:1, :], writes=(c_b,))
    sch.I("pool", "memset", (), (c_b,), zer[:, :], 0.0)
    sch.I("pool", "memset", (), (c_b,), selz[:, 0:64], 0.0)
    sch.I("pool", "memset", (), (c_b,), selz[:, 64:128], 1.0)
    sch.I("act", "activation", (c_b,), (c_b,), out=es[:, :], in_=es[:, :], func=AF.Exp)
    for hd in range(8):
        sch.I("dve", "tensor_scalar", (c_b,), (c_b,), out=esr[:, hd // 4, (hd % 4) * 128:(hd % 4 + 1) * 128],
              in0=zer[:, :], scalar1=es[:, hd:hd + 1], scalar2=None, op0=ALU.add)
    Ov = O.rearrange("(h d) s -> d h s", d=64)
    steps = []
    for n in range(NB):
        for g_ in range(2):
            tiles = ([(0, n - 1)] if n > 0 else []) + [(1, n)]
            for ti, (typ, kt) in enumerate(tiles):
                steps.append((n, g_, typ, kt, ti == 0, ti == len(tiles) - 1))
    state = {}

    def qk(idx):
        n, g_, typ, kt, first, last = steps[idx]
        k = idx % NSB
        p, pb = ps_s.next()
        kk = k_c[g_] if typ == 1 else k_p[g_]
        sch.mm(p[:, 0:512], kk[:, kt * 128:(kt + 1) * 128], q_t[g_][:, n * 512:(n + 1) * 512], True, False,
               (qkv_b,), (pb,))
        sch.mm(p[:, 0:512], negi[:, :], mask[:, typ, :], False, True, (c_b,), (pb,))
        sch.I("act", "activation", (pb,), (pt_b[k],), out=pt[k][:, :], in_=p[:, 0:512], func=AF.Exp)

    def pv(idx):
        n, g_, typ, kt, first, last = steps[idx]
        k = idx % NSB
        if first:
            state["po"], state["po_b"] = ps_o.next()
            sch.mm(state["po"][:, 0:512], selz[:, :], esr[:, g_, :], True, False, (c_b,), (state["po_b"],))
        po, po_b = state["po"], state["po_b"]
        sch.mm(po[:, 0:512], v_t[:, kt, g_, :], pt[k][:, :], False, last, (qkv_b, pt_b[k]), (po_b,))
        if last:
            e = (n * 2 + g_) % 2
            sch.I("act", "activation", (po_b,), (den_b[e],), out=rden[e][0:64, :], in_=po[64:128, 0:512], func=AF.Ln)
            sch.I("act", "activation", (den_b[e],), (den_b[e],), out=rden[e][0:64, :], in_=rden[e][0:64, :], func=AF.Exp,
                  scale=-1.0)
            sch.I("dve", "tensor_tensor", (po_b, den_b[e]), (ot_b[e],), out=ot[e][:, :], in0=po[0:64, 0:512], in1=rden[e][:, :],
                  op=ALU.mult)
            sch.dma(Ov[:, g_ * 4:(g_ + 1) * 4, n * 128:(n + 1) * 128], ot[e][:, :].rearrange("d (h q) -> d h q", q=128),
                    reads=(ot_b[e],))

    LOOK = 2
    for idx in range(len(steps) + LOOK):
        if idx < len(steps):
            qk(idx)
        if idx - LOOK >= 0:
            pv(idx - LOOK)
    sch.flush()
    cx.close()


def phase_attn_causal(nc, sch, S, A, H, KD, QT, KT, KR, V, scale, o_row0):
    QC = 512 if S >= 512 else S
    NCH = S // QC
    NB = S // 128
    KPC = QC // 128
    cx = Ctx(nc, sch)
    O = A["O"]
    q_t = [cx.sb([KD, S], BF16, "q") for _ in range(2)]
    k_t = [cx.sb([KD, S], BF16, "k") for _ in range(2)]
    v_t = [cx.sb([128, NB, 2, 128], BF16, "v") for _ in range(2)]
    q_b, k_b, v_b = sch.bufs_n("q", 2), sch.bufs_n("k", 2), sch.bufs_n("v", 2)
    tri = cx.sb([128, 128], BF16, "tri")
    c_b = sch.buf("const")
    PAIR = False
    ps_s = PsRing(cx, sch, 4, "ps_s")
    ps_o = PsRing(cx, sch, 3, "ps_o")
    NPT = 5
    pt = [cx.sb([128, 512], BF16, "pt") for _ in range(NPT)]
    pt_b = sch.bufs_n("pt", NPT)
    rden = [cx.sb([64, 512], F32, "rden") for _ in range(2)]
    ot = [cx.sb([64, 512], BF16, "ot") for _ in range(2)]
    rden_b, ot_b = sch.bufs_n("rden", 2), sch.bufs_n("ot", 2)
    for j_ in range(2):
        sch.I("pool", "memset", (), (v_b[j_],), v_t[j_][:, :, :, 64:128], 1.0)
    sch.dma(tri[:, :], A["tri"], writes=(c_b,))
    Vv = V.rearrange("(n p) f -> p n f", p=128)

    def load_head(h):
        i = h % 2
        sch.dma(q_t[i][:, :], QT[h, :, :], writes=(q_b[i],))
        if KR is None:
            sch.dma(k_t[i][:, :], KT[h, :, :], writes=(k_b[i],))
        else:
            sch.dma(k_t[i][0:64, :], KT[h, :, :], writes=(k_b[i],))
            sch.dma(k_t[i][64:96, :], KR[:, :], writes=(k_b[i],))
        if h % 2 == 0:
            j = (h // 2) % 2
            for hh_ in range(2):
                sch.dma(v_t[j][:, :, hh_, 0:64], Vv[:, :, (h + hh_) * 64:(h + hh_ + 1) * 64], writes=(v_b[j],))

    steps = []
    for h in range(H):
        for c in range(NCH):
            nk = KPC * (c + 1)
            nfull = KPC * c
            j = 0
            while j < nk:
                if PAIR and j + 1 < nfull:
                    steps.append((h, c, (j, j + 1), nk))
                    j += 2
                else:
                    steps.append((h, c, (j,), nk))
                    j += 1
    load_head(0)
    state = {}

    def qk(idx):
        h, c, js, nk = steps[idx]
        i = h % 2
        p, pb = ps_s.next()
        k = idx % NPT
        if len(js) == 2:
            for u, j in enumerate(js):
                sch.mm(p[:, u * 512:u * 512 + QC], k_t[i][:, j * 128:(j + 1) * 128], q_t[i][:, c * QC:(c + 1) * QC], True, True,
                       (q_b[i], k_b[i]), (pb,))
            sch.I("act", "activation", (pb,), (pt_b[k],), out=pt[k][:, 0:1024], in_=p[:, 0:1024], func=AF.Exp,
                  scale=float(scale))
            return
        j = js[0]
        jj = j - KPC * c
        q0 = 128 * jj if jj >= 0 else 0
        n = QC - q0
        sch.mm(p[:, 0:n], k_t[i][:, j * 128:(j + 1) * 128], q_t[i][:, c * QC + q0:(c + 1) * QC], True, True,
               (q_b[i], k_b[i]), (pb,))
        sch.I("act", "activation", (pb,), (pt_b[k],), out=pt[k][:, 0:n], in_=p[:, 0:n], func=AF.Exp, scale=float(scale))
        if jj >= 0:
            sch.I("pool", "tensor_tensor", (pt_b[k], c_b), (pt_b[k],), out=pt[k][:, 0:128], in0=pt[k][:, 0:128], in1=tri[:, :],
                  op=ALU.mult)

    def pv(idx):
        h, c, js, nk = steps[idx]
        i = h % 2
        k = idx % NPT
        if js[0] == 0:
            state["po"], state["po_b"] = ps_o.next()
        po, po_b = state["po"], state["po_b"]
        vj = (h // 2) % 2
        for u, j in enumerate(js):
            jj = j - KPC * c
            q0 = 128 * jj if jj >= 0 else 0
            n = QC - q0
            src = pt[k][:, u * 512:u * 512 + n] if len(js) == 2 else pt[k][:, 0:n]
            sch.mm(po[:, q0:QC], v_t[vj][:, j, h % 2, :], src, j == 0, j == nk - 1, (v_b[vj], pt_b[k]), (po_b,))
        if js[-1] == nk - 1:
            e = (h * NCH + c) % 2
            sch.I("dve", "reciprocal", (po_b,), (rden_b[e],), out=rden[e][:, 0:QC], in_=po[64:128, 0:QC])
            sch.I("dve", "tensor_tensor", (po_b, rden_b[e]), (ot_b[e],), out=ot[e][:, 0:QC], in0=po[0:64, 0:QC],
                  in1=rden[e][:, 0:QC], op=ALU.mult)
            sch.dma(O[o_row0 + h * 64:o_row0 + (h + 1) * 64, c * QC:(c + 1) * QC], ot[e][:, 0:QC], reads=(ot_b[e],))

    LOOK = 3
    for idx in range(len(steps) + LOOK):
        if idx < len(steps):
            h, c, js, nk = steps[idx]
            if c == 0 and js[0] == 0 and h + 1 < H:
                load_head(h + 1)
            qk(idx)
        if idx - LOOK >= 0:
            pv(idx - LOOK)
    sch.flush()
    cx.close()


def phase_oproj(nc, sch, S, h_in, h_out, w_out, A):
    N = 512 if S >= 512 else S
    NT = S // N
    NH = 3
    cx = Ctx(nc, sch)
    w = cx.sb([128, KC, D], BF16, "wo")
    h_t = [cx.sb([128, KC, N], F32, "h") for _ in range(NH)]
    h_b = sch.bufs_n("h", NH)
    o_t = [cx.sb([128, KC, N], BF16, "o") for _ in range(NH)]
    o_b = sch.bufs_n("o", NH)
    pr = PsRing(cx, sch, 4)
    hv = h_in.rearrange("(c p) s -> p c s", p=128)
    ho = h_out.rearrange("(c p) s -> p c s", p=128)
    Ov = A["O"].rearrange("(c p) s -> p c s", p=128)

    def load(t):
        if t < NT:
            sch.dma(o_t[t % NH][:, :, :], Ov[:, :, t * N:(t + 1) * N], writes=(o_b[t % NH],))
            sch.dma(h_t[t % NH][:, :, :], hv[:, :, t * N:(t + 1) * N], writes=(h_b[t % NH],))

    load(0)
    load(1)
    wl = WLoader(cx, sch, stage_elems=2048)
    w_r = wl.load(w, w_out, KC, D)
    for t in range(NT):
        i = t % NH
        load(t + 2)
        for c in range(KC):
            p, pb = pr.next()
            for f in range(KC):
                sch.mm(p[:, 0:N], w[:, f, c * 128:(c + 1) * 128], o_t[i][:, f, :], f == 0, f == KC - 1,
                       w_r.cols(c * 128, (c + 1) * 128) + (o_b[i],), (pb,))
            sch.I("dve", "tensor_tensor", (pb, h_b[i]), (h_b[i],), out=h_t[i][:, c, :], in0=p[:, 0:N], in1=h_t[i][:, c, :],
                  op=ALU.add)
        sch.dma(ho[:, :, t * N:(t + 1) * N], h_t[i][:, :, :], reads=(h_b[i],))
    sch.flush()
    cx.close()


def phase_ple(nc, sch, S, h_in, h_out, pT, wg_d, wp_d, g_d, final_g=None):
    N = 512 if S >= 512 else S
    NT = S // N
    NH = 3
    cx = Ctx(nc, sch)
    wg = cx.sb([128, KC, D], BF16, "wg")
    wp = cx.sb([128, 2, D], BF16, "wp")
    g_sb = cx.sb([128, KC], F32, "g")
    fg_sb = cx.sb([128, KC], F32, "fg")
    g_b = sch.buf("g")
    T = norm_tiles(cx, sch, KC, N)
    h_t = [cx.sb([128, KC, N], F32, "h") for _ in range(NH)]
    h_b = sch.bufs_n("h", NH)
    p_t = [cx.sb([128, 2, N], F32, "p") for _ in range(2)]
    p_b = sch.bufs_n("p", 2)
    pbf = [cx.sb([128, 2, N], BF16, "pbf") for _ in range(2)]
    pbf_b = sch.bufs_n("pbf", 2)
    xn2 = [cx.sb([128, KC, N], BF16, "xn") for _ in range(2)]
    xn2_b = sch.bufs_n("xn", 2)
    sg = [cx.sb([128, N], F32, "sg") for _ in range(2)]
    tt = [cx.sb([128, N], F32, "tt") for _ in range(2)]
    sg_b, tt_b = sch.bufs_n("sg", 2), sch.bufs_n("tt", 2)
    outt = [cx.sb([128, KC, N], F32, "outt") for _ in range(2)] if final_g is not None else None
    outt_b = sch.bufs_n("outt", 2)
    prg = PsRing(cx, sch, 3, "pg")
    prp = PsRing(cx, sch, 3, "pp")
    hv = h_in.rearrange("(c p) s -> p c s", p=128)
    ho = h_out.rearrange("(c p) s -> p c s", p=128)
    pv = pT.rearrange("(c p) s -> p c s", p=128)
    sch.dma(g_sb[:, :], g_d, writes=(g_b,))
    if final_g is not None:
        sch.dma(fg_sb[:, :], final_g, writes=(g_b,))
    sch.dma(h_t[0][:, :, :], hv[:, :, 0:N], writes=(h_b[0],))
    sch.dma(p_t[0][:, :, :], pv[:, :, 0:N], writes=(p_b[0],))

    def norm(t):
        rmsnorm_stats(sch, h_t[t % NH][:, :, :], h_b[t % NH], KC, N, D, T)
        apply_norm(sch, xn2[t % 2], xn2_b[t % 2], h_t[t % NH], h_b[t % NH], g_sb, g_b, T, KC, N)

    def final_norm(t):
        hi, oi_ = t % NH, t % 2
        rmsnorm_stats(sch, h_t[hi][:, :, :], h_b[hi], KC, N, D, T)
        apply_norm(sch, outt[oi_], outt_b[oi_], h_t[hi], h_b[hi], fg_sb, g_b, T, KC, N)
        sch.dma(ho[:, :, t * N:(t + 1) * N], outt[oi_][:, :, :], reads=(outt_b[oi_],))

    norm(0)
    if NT > 1:
        sch.dma(h_t[1][:, :, :], hv[:, :, N:2 * N], writes=(h_b[1],))
    wl = WLoader(cx, sch, stage_elems=2048)
    wg_r = wl.load(wg, wg_d, KC, D)
    wp_r = wl.load(wp, wp_d, 2, D)
    for t in range(NT):
        i = t % NH
        ip = t % 2
        if t + 1 < NT:
            sch.dma(p_t[1 - ip][:, :, :], pv[:, :, (t + 1) * N:(t + 2) * N], writes=(p_b[1 - ip],))
        def load_ahead():
            if t + 2 < NT:
                sch.dma(h_t[(t + 2) % NH][:, :, :], hv[:, :, (t + 2) * N:(t + 3) * N], writes=(h_b[(t + 2) % NH],))

        if final_g is None or t == 0:
            load_ahead()
        xn, xn_b = xn2[t % 2], xn2_b[t % 2]
        sch.I("act", "copy", (p_b[ip],), (pbf_b[ip],), out=pbf[ip][:, :, :], in_=p_t[ip][:, :, :])
        for c in range(KC):
            j = c % 2
            pgt, pgb = prg.next()
            ppt, ppb = prp.next()
            for f in range(KC):
                sch.mm(pgt[:, 0:N], wg[:, f, c * 128:(c + 1) * 128], xn[:, f, :], f == 0, f == KC - 1,
                       wg_r.cols(c * 128, (c + 1) * 128) + (xn_b,), (pgb,))
            for f in range(2):
                sch.mm(ppt[:, 0:N], wp[:, f, c * 128:(c + 1) * 128], pbf[ip][:, f, :], f == 0, f == 1,
                       wp_r.cols(c * 128, (c + 1) * 128) + (pbf_b[ip],), (ppb,))
            sch.I("act", "activation", (pgb,), (sg_b[j],), out=sg[j][:, :], in_=pgt[:, 0:N], func=AF.Sigmoid)
            sch.I("dve", "tensor_tensor", (ppb, sg_b[j]), (tt_b[j],), out=tt[j][:, :], in0=ppt[:, 0:N], in1=sg[j][:, :],
                  op=ALU.mult)
            sch.I("dve", "tensor_tensor", (tt_b[j], h_b[i]), (h_b[i],), out=h_t[i][:, c, :], in0=h_t[i][:, c, :],
                  in1=tt[j][:, :], op=ALU.add)
            if c == 1 and final_g is not None and t > 0:
                final_norm(t - 1)
                load_ahead()
            if c == 4 and t + 1 < NT:
                norm(t + 1)
        if final_g is None:
            sch.dma(ho[:, :, t * N:(t + 1) * N], h_t[i][:, :, :], reads=(h_b[i],))
    if final_g is not None:
        final_norm(NT - 1)
    sch.flush()
    cx.close()


IN_SPECS = {}


def _in_specs(S):
    sp = {"xT": ([D, S], F32), "pT": ([2, PLE, S], F32), "final_g": ([128, KC], F32),
          "rope_cs": ([32, 2, S], F32), "tri": ([128, 128], BF16), "swa_qaug": ([2, 2, S * 4], BF16), "swa_kaug": ([2, 2, S], BF16), "swa_mask": ([128, 2, 512], BF16), "negi": ([128, 128], BF16),
          "ev_w_in": ([128, KC, 1184], F32), "ev_sinks": ([64, 8], F32), "ev_cq_g": ([128, 2], F32),
          "ev_w_uq": ([128, 2, 768], F32), "ev_ckv_g": ([128, 1], F32), "ev_w_ukv": ([128, 1, 1024], F32),
          "ev_w_out": ([128, KC, D], F32), "od_w_in": ([128, KC, 3088], F32), "od_bf": ([16, 1], F32),
          "od_w_out": ([128, KC, D], F32)}
    for l in range(2):
        for ab in ("a", "b"):
            sp["ff%s_wgu%d" % (ab, l)] = ([128, FC, KC, 256], F32)
            sp["ff%s_wd%d" % (ab, l)] = ([128, FC, D], F32)
            sp["ff%s_g%d" % (ab, l)] = ([128, KC], F32)
        sp["mix_g%d" % l] = ([128, KC], F32)
        sp["ple_g%d" % l] = ([128, KC], F32)
        sp["ple_wg%d" % l] = ([128, KC, D], F32)
        sp["ple_wp%d" % l] = ([128, 2, D], F32)
    return sp


SCRATCH = lambda S: {"hbuf": ([D, S], F32), "QS": ([2, 64, S * 4], BF16), "KS": ([128, S], BF16), "VS": ([S, 128], BF16),
                     "QM": ([8, 96, S], BF16), "KM": ([8, 64, S], BF16), "KR": ([32, S], BF16), "VM": ([S, 512], BF16),
                     "O": ([D, S], BF16), "QF": ([16, 70, S], BF16), "KF": ([16, 70, S], BF16), "VF": ([S, D], BF16)}


def build(S, stop_after=None, dbg=False):
    nc = bass.Bass("TRN2", target_bir_lowering=False)
    stack = contextlib.ExitStack()
    sch = Sched(nc, stack)
    A = {}
    for name, (shape, dtype) in _in_specs(S).items():
        A[name] = nc.dram_tensor(name, list(shape), dtype, kind="ExternalInput").ap()
    for name, (shape, dtype) in SCRATCH(S).items():
        A[name] = nc.dram_tensor(name, list(shape), dtype, kind="ExternalOutput" if dbg else "Internal").ap()
    outT = nc.dram_tensor("outT", [D, S], F32, kind="ExternalOutput").ap()
    h = A["hbuf"]
    ph = []
    ph.append(("ffa0", lambda: phase_ffn(nc, sch, S, A["xT"], h, A["ffa_wgu0"], A["ffa_wd0"], A["ffa_g0"])))
    ph.append(("proj0", lambda: phase_proj_even(nc, sch, S, h, A)))
    ph.append(("swa", lambda: phase_attn_swa(nc, sch, S, A)))
    ph.append(("mla", lambda: phase_attn_causal(nc, sch, S, A, 8, 96, A["QM"], A["KM"], A["KR"], A["VM"], 96.0 ** -0.5, 512)))
    ph.append(("oproj0", lambda: phase_oproj(nc, sch, S, h, h, A["ev_w_out"], A)))
    ph.append(("ffb0", lambda: phase_ffn(nc, sch, S, h, h, A["ffb_wgu0"], A["ffb_wd0"], A["ffb_g0"])))
    ph.append(("ple0", lambda: phase_ple(nc, sch, S, h, h, A["pT"][0], A["ple_wg0"], A["ple_wp0"], A["ple_g0"])))
    ph.append(("ffa1", lambda: phase_ffn(nc, sch, S, h, h, A["ffa_wgu1"], A["ffa_wd1"], A["ffa_g1"])))
    ph.append(("proj1", lambda: phase_proj_odd(nc, sch, S, h, A)))
    ph.append(("fox", lambda: phase_attn_causal(nc, sch, S, A, 16, 70, A["QF"], A["KF"], None, A["VF"], 1.0, 0)))
    ph.append(("oproj1", lambda: phase_oproj(nc, sch, S, h, h, A["od_w_out"], A)))
    ph.append(("ffb1", lambda: phase_ffn(nc, sch, S, h, h, A["ffb_wgu1"], A["ffb_wd1"], A["ffb_g1"])))
    ph.append(("ple1", lambda: phase_ple(nc, sch, S, h, outT, A["pT"][1], A["ple_wg1"], A["ple_wp1"], A["ple_g1"],
                                         final_g=A["final_g"])))
    for name, fn in ph:
        fn()
        if stop_after == name:
            break
    stack.close()
    return nc


def tile_w(w):
    K, M = w.shape
    return np.ascontiguousarray(w.reshape(K // 128, 128, M).transpose(1, 0, 2))


def tile_wgu(w):
    g_ = w[:, :DFF].reshape(KC, 128, FC, 128)
    u_ = w[:, DFF:].reshape(KC, 128, FC, 128)
    gu = np.concatenate([g_, u_], axis=3)
    return np.ascontiguousarray(gu.transpose(1, 2, 0, 3))


def tile_g(g):
    return np.ascontiguousarray(g.reshape(-1, 128).T)


def const_tables(S):
    inv = (10000.0 ** (-np.arange(0, 32, 2, dtype=np.float32) / np.float32(32))).astype(np.float32)
    ang = (np.arange(S, dtype=np.float32)[:, None] * inv[None, :]).astype(np.float32)
    cos, sin = np.cos(ang).astype(np.float32), np.sin(ang).astype(np.float32)
    cos2 = np.concatenate([cos, cos], 1).T
    sin2 = np.concatenate([sin, sin], 1).T
    rope_cs = np.ascontiguousarray(np.stack([cos2, sin2], 1)).astype(np.float32)
    ki = np.arange(128)[:, None]
    qi = np.arange(128)[None, :]
    tri = (ki <= qi).astype(np.float32).astype(ml_dtypes.bfloat16)
    slopes = (2.0 ** (-8.0 * np.arange(1, 9, dtype=np.float32) / 8)).astype(np.float32)
    NB = S // 128
    qaug = np.zeros((2, 2, NB, 4, 128), np.float32)
    for g_ in range(2):
        for hh in range(4):
            sl = slopes[g_ * 4 + hh]
            qaug[g_, 0, :, hh, :] = -sl * np.arange(128, dtype=np.float32)[None, :]
            qaug[g_, 1, :, hh, :] = sl
    qaug = qaug.reshape(2, 2, S * 4).astype(ml_dtypes.bfloat16)
    kaug = np.zeros((2, 2, S), np.float32)
    kin = (np.arange(S) % 128).astype(np.float32)
    kaug[:, 0, :] = 1.0
    kaug[1, 1, :] = kin
    kaug[0, 1, :] = kin - 128.0
    kaug = kaug.astype(ml_dtypes.bfloat16)
    mask = np.zeros((128, 2, 512), np.float32)
    for hh in range(4):
        mask[:, 1, hh * 128:(hh + 1) * 128] = (ki > qi)
        mask[:, 0, hh * 128:(hh + 1) * 128] = (ki <= qi)
    mask = mask.astype(ml_dtypes.bfloat16)
    negi = (-256.0 * np.eye(128, dtype=np.float32)).astype(ml_dtypes.bfloat16)
    return rope_cs, tri, qaug, kaug, mask, negi


def host_inputs(inputs, S):
    f = lambda a: np.ascontiguousarray(np.asarray(a, dtype=np.float32))
    rope_cs, tri, qaug, kaug, mask, negi = const_tables(S)
    sh = {"rope_cs": rope_cs, "tri": tri, "swa_qaug": qaug, "swa_kaug": kaug, "swa_mask": mask, "negi": negi,
          "final_g": tile_g(f(inputs["final_norm"]))}
    ffw = {"a": (inputs["ffa_w_gate_up"], inputs["ffa_w_down"], inputs["ffa_norm"]),
           "b": (inputs["ffb_w_gate_up"], inputs["ffb_w_down"], inputs["ffb_norm"])}
    for l in range(2):
        for ab in ("a", "b"):
            sh["ff%s_wgu%d" % (ab, l)] = tile_wgu(f(ffw[ab][0][l]))
            sh["ff%s_wd%d" % (ab, l)] = tile_w(f(ffw[ab][1][l]))
            sh["ff%s_g%d" % (ab, l)] = tile_g(f(ffw[ab][2][l]))
        sh["mix_g%d" % l] = tile_g(f(inputs["mix_norm"][l]))
        sh["ple_g%d" % l] = tile_g(f(inputs["ple_norm"][l]))
        sh["ple_wg%d" % l] = tile_w(f(inputs["ple_w_gate"][l]))
        sh["ple_wp%d" % l] = tile_w(f(inputs["ple_w_proj"][l]))
    sh["ev_w_in"] = tile_w(f(inputs["ev_w_in"][0]))
    sh["ev_sinks"] = np.ascontiguousarray(np.tile(f(inputs["ev_sinks"][0])[None, :], (64, 1)))
    sh["ev_cq_g"] = tile_g(f(inputs["ev_cq_norm"][0]))
    sh["ev_w_uq"] = tile_w(f(inputs["ev_w_uq"][0]))
    sh["ev_ckv_g"] = tile_g(f(inputs["ev_ckv_norm"][0]))
    sh["ev_w_ukv"] = tile_w(f(inputs["ev_w_ukv"][0]))
    sh["ev_w_out"] = tile_w(f(inputs["ev_w_out"][0]))
    sh["od_w_in"] = tile_w(f(inputs["od_w_in"][0]))
    sh["od_bf"] = f(inputs["od_b_f"][0]).reshape(16, 1)
    sh["od_w_out"] = tile_w(f(inputs["od_w_out"][0]))
    x = f(inputs["x"])
    p = f(inputs["p"])
    B = x.shape[0]
    per = []
    for b in range(B):
        m = dict(sh)
        m["xT"] = np.ascontiguousarray(x[b].T)
        m["pT"] = np.ascontiguousarray(p[:, b].transpose(0, 2, 1))
        per.append(m)
    return per


_NC_CACHE = {}


def kernel(**inputs):
    x = np.asarray(inputs["x"])
    B, S, _ = x.shape
    if S not in _NC_CACHE:
        _NC_CACHE[S] = build(S)
    nc = _NC_CACHE[S]
    in_maps = host_inputs(inputs, S)
    res = run_bass_kernel_spmd(nc, in_maps, core_ids=list(range(B)))
    out = np.stack([np.ascontiguousarray(r["outT"].T) for r in res.results], 0)
    return out.astype(np.float32)
```
